# Optimizing a Trainium2 kernel written in Bass

```python
import jax, jax.numpy as jnp
from jax import lax
import numpy as np

D_MODEL = 4096
BATCH = 2
SEQ = 4096
DEPTH = 2

D_MIX = D_MODEL
HEAD_DIM = 128
A_GROUPS = 8
A_CH = 128
A_WIDTH = A_GROUPS * A_CH
CHUNK = 128
B_HEADS = 12
B_WIDTH = B_HEADS * HEAD_DIM
C_HEADS = 12
C_NOPE = 128
C_ROPE = 64
C_VDIM = 128
C_WIDTH = C_HEADS * C_VDIM
Q_LORA = 768
KV_LORA = 512
ROPE_THETA = 10000.0
D_IN = 3 * A_WIDTH + 4 * B_WIDTH + Q_LORA + KV_LORA + C_ROPE + C_WIDTH
EPS = 1e-6
Q_BLOCK = 128

kernel_name = "hybrid_gmlp_stickbreak_mla_parallel_heads"


def rmsnorm(x, g):
    xf = x.astype(jnp.float32)
    xf = xf * lax.rsqrt(jnp.mean(xf * xf, axis=-1, keepdims=True) + EPS)
    return (xf * g.astype(jnp.float32)).astype(x.dtype)


def gated_norm(y, g, z):
    return rmsnorm(y, g) * jax.nn.silu(z)


def rope(x, cos, sin):
    half = x.shape[-1] // 2
    x1, x2 = x[..., :half], x[..., half:]
    out = jnp.concatenate([x1 * cos - x2 * sin, x2 * cos + x1 * sin], axis=-1)
    return out.astype(x.dtype)


def chunked_gmlp(u, v, g_v, w_s, b_s):
    bn, s, _ = u.shape
    u = jax.nn.gelu(u)
    v = rmsnorm(jax.nn.gelu(v).reshape(bn, s, A_GROUPS, A_CH), g_v)
    v = v.reshape(bn, s // CHUNK, CHUNK, A_GROUPS, A_CH)
    causal = jnp.tril(jnp.ones((CHUNK, CHUNK), dtype=bool))
    w = jnp.where(causal[None], w_s, 0.0).astype(v.dtype)
    sv = jnp.einsum('gts,bcsgd->bctgd', w, v) + b_s.T[None, None, :, :, None]
    return u * sv.reshape(bn, s, A_WIDTH)


def stick_breaking(q, k, v):
    bn, s, h, d = q.shape
    nb = s // Q_BLOCK
    scale = d ** -0.5
    kpos = jnp.arange(s)
    qb = q.reshape(bn, nb, Q_BLOCK, h, d).transpose(1, 0, 2, 3, 4)

    def block(args):
        qi, bi = args
        z = jnp.einsum('bthd,bshd->bhts', qi, k).astype(jnp.float32) * scale
        qpos = bi * Q_BLOCK + jnp.arange(Q_BLOCK)
        strict = kpos[None, :] < qpos[:, None]
        log_keep = jnp.where(strict, jax.nn.log_sigmoid(-z), 0.0)
        after = lax.cumsum(log_keep, axis=3, reverse=True) - log_keep
        a = jnp.where(strict, jnp.exp(jax.nn.log_sigmoid(z) + after), 0.0)
        return jnp.einsum('bhts,bshd->bthd', a.astype(v.dtype), v)

    out = lax.map(block, (qb, jnp.arange(nb)))
    return out.transpose(1, 0, 2, 3, 4).reshape(bn, s, h * d)


def mla(c_q, c_kv, k_rope, cos, sin, g_q, g_kv, w_uq, w_ukv):
    bn, s, _ = c_q.shape
    q = jnp.einsum('bsr,rn->bsn', rmsnorm(c_q, g_q), w_uq).reshape(bn, s, C_HEADS, C_NOPE + C_ROPE)
    q_nope = q[..., :C_NOPE]
    q_rope = rope(q[..., C_NOPE:], cos, sin)
    kv = jnp.einsum('bsr,rn->bsn', rmsnorm(c_kv, g_kv), w_ukv).reshape(bn, s, C_HEADS, C_NOPE + C_VDIM)
    k_nope, v = kv[..., :C_NOPE], kv[..., C_NOPE:]
    k_r = rope(k_rope, cos[:, :, 0], sin[:, :, 0])
    scale = (C_NOPE + C_ROPE) ** -0.5
    nb = s // Q_BLOCK
    kpos = jnp.arange(s)
    qn_b = q_nope.reshape(bn, nb, Q_BLOCK, C_HEADS, C_NOPE).transpose(1, 0, 2, 3, 4)
    qr_b = q_rope.reshape(bn, nb, Q_BLOCK, C_HEADS, C_ROPE).transpose(1, 0, 2, 3, 4)

    def block(args):
        qn, qr, bi = args
        z = (jnp.einsum('bthd,bshd->bhts', qn, k_nope)
             + jnp.einsum('bthr,bsr->bhts', qr, k_r)).astype(jnp.float32) * scale
        qpos = bi * Q_BLOCK + jnp.arange(Q_BLOCK)
        causal = kpos[None, :] <= qpos[:, None]
        p = jax.nn.softmax(jnp.where(causal, z, -jnp.inf), axis=-1)
        return jnp.einsum('bhts,bshd->bthd', p.astype(v.dtype), v)

    out = lax.map(block, (qn_b, qr_b, jnp.arange(nb)))
    return out.transpose(1, 0, 2, 3, 4).reshape(bn, s, C_WIDTH)


def hybrid_layer(x, cos, sin, g_pre, w_in, a_g_v, a_w_s, a_b_s,
                 c_g_q, c_g_kv, c_w_uq, c_w_ukv, g_out, w_out):
    bn, s, _ = x.shape
    h = rmsnorm(x, g_pre)
    proj = jnp.einsum('bsd,dn->bsn', h, w_in)
    sizes = [A_WIDTH, A_WIDTH, A_WIDTH, B_WIDTH, B_WIDTH, B_WIDTH, B_WIDTH,
             Q_LORA, KV_LORA, C_ROPE, C_WIDTH]
    offsets, acc = [], 0
    for sz in sizes[:-1]:
        acc += sz
        offsets.append(acc)
    u_a, v_a, z_a, q_b, k_b, v_b, z_b, cq, ckv, kr, z_c = jnp.split(proj, offsets, axis=-1)

    y_a = chunked_gmlp(u_a, v_a, a_g_v, a_w_s, a_b_s)
    y_b = stick_breaking(q_b.reshape(bn, s, B_HEADS, HEAD_DIM),
                         k_b.reshape(bn, s, B_HEADS, HEAD_DIM),
                         v_b.reshape(bn, s, B_HEADS, HEAD_DIM))
    y_c = mla(cq, ckv, kr, cos, sin, c_g_q, c_g_kv, c_w_uq, c_w_ukv)

    y = jnp.concatenate([
        gated_norm(y_a, g_out[:A_WIDTH], z_a),
        gated_norm(y_b, g_out[A_WIDTH:A_WIDTH + B_WIDTH], z_b),
        gated_norm(y_c, g_out[A_WIDTH + B_WIDTH:], z_c),
    ], axis=-1)
    return x + jnp.einsum('bsn,nd->bsd', y, w_out)


def setup_inputs(seed: int = 0) -> dict:
    key = jax.random.key(seed)
    ks = jax.random.split(key, 16)
    f32 = jnp.float32

    def nrm(k, shape, scale):
        return jax.random.normal(k, shape, f32) * scale

    x = jax.random.normal(ks[0], (BATCH, SEQ, D_MODEL), f32)
    offset = jax.random.randint(ks[1], (BATCH, 1), 0, 1024, dtype=jnp.int32)
    positions = (offset + jnp.arange(SEQ, dtype=jnp.int32)[None, :]).astype(jnp.int32)
    return {
        "x": x,
        "positions": positions,
        "g_pre": 1.0 + nrm(ks[2], (DEPTH, D_MODEL), 0.02),
        "w_in": nrm(ks[3], (DEPTH, D_MODEL, D_IN), D_MODEL ** -0.5),
        "a_g_v": 1.0 + nrm(ks[4], (DEPTH, A_GROUPS, A_CH), 0.02),
        "a_w_s": nrm(ks[5], (DEPTH, A_GROUPS, CHUNK, CHUNK), CHUNK ** -0.5),
        "a_b_s": 1.0 + nrm(ks[6], (DEPTH, A_GROUPS, CHUNK), 0.02),
        "c_g_q": 1.0 + nrm(ks[7], (DEPTH, Q_LORA), 0.02),
        "c_g_kv": 1.0 + nrm(ks[8], (DEPTH, KV_LORA), 0.02),
        "c_w_uq": nrm(ks[9], (DEPTH, Q_LORA, C_HEADS * (C_NOPE + C_ROPE)), Q_LORA ** -0.5),
        "c_w_ukv": nrm(ks[10], (DEPTH, KV_LORA, C_HEADS * (C_NOPE + C_VDIM)), KV_LORA ** -0.5),
        "g_out": 1.0 + nrm(ks[11], (DEPTH, D_MIX), 0.02),
        "w_out": nrm(ks[12], (DEPTH, D_MIX, D_MODEL), D_MIX ** -0.5),
        "g_final": 1.0 + nrm(ks[13], (D_MODEL,), 0.02),
    }


def reference(x, positions, g_pre, w_in, a_g_v, a_w_s, a_b_s, c_g_q, c_g_kv,
              c_w_uq, c_w_ukv, g_out, w_out, g_final):
    inv_freq = 1.0 / (ROPE_THETA ** (jnp.arange(0, C_ROPE, 2, dtype=jnp.float32) / C_ROPE))
    ang = positions.astype(jnp.float32)[..., None] * inv_freq
    cos = jnp.cos(ang)[:, :, None, :]
    sin = jnp.sin(ang)[:, :, None, :]
    h = x
    for l in range(DEPTH):
        h = hybrid_layer(h, cos, sin, g_pre[l], w_in[l], a_g_v[l], a_w_s[l], a_b_s[l],
                         c_g_q[l], c_g_kv[l], c_w_uq[l], c_w_ukv[l], g_out[l], w_out[l])
    return rmsnorm(h, g_final)
```

```python
import contextlib
import os
import numpy as np
import ml_dtypes
import concourse.bass as bass
import concourse.mybir as mybir
from concourse.bass_utils import run_bass_kernel_spmd

F32 = mybir.dt.float32
BF16 = mybir.dt.bfloat16
I32 = mybir.dt.int32
AF = mybir.ActivationFunctionType
ALU = mybir.AluOpType

NCORE = 8
S = 4096
D = 4096
DEPTH = 2
TL = 1024
NT = 8
H = 12
EPS = 1e-6
SCALE_B = 128 ** -0.5
SCALE_C = 192 ** -0.5
WB = 256


class Tok:
    __slots__ = ("name", "writers", "readers", "gdeps", "sem", "cnt", "excl")

    def __init__(self, name, excl=False):
        self.name = name
        self.excl = excl
        self.writers = []
        self.readers = []
        self.gdeps = []
        self.sem = None
        self.cnt = 0


class Ins:
    __slots__ = ("eng", "fn", "deps", "dma", "sig", "sem", "val", "tok", "idx", "cc")

    def __init__(self, eng, fn, dma, cc):
        self.eng = eng
        self.fn = fn
        self.deps = []
        self.dma = dma
        self.cc = cc
        self.sig = False
        self.sem = None
        self.val = 0
        self.tok = None
        self.idx = 0


ENGS = ("pe", "act", "dve", "pool", "sp")


class Prog:
    def __init__(self):
        self.streams = {e: [] for e in ENGS}
        self.n = 0

    def op(self, eng, fn, reads=(), writes=(), partial=False, dma=False, cc=False):
        ins = Ins(eng, fn, dma, cc)
        ins.idx = self.n
        self.n += 1
        deps = []
        for t in reads:
            deps.extend(t.writers)
            if t.excl:
                deps.extend(r for r in t.readers if r.eng != eng)
        for t in writes:
            if partial and not t.readers and t.writers:
                deps.extend(t.gdeps)
                t.writers.append(ins)
            else:
                g = t.readers + t.writers
                deps.extend(g)
                t.gdeps = g
                t.writers = [ins]
                t.readers = []
        for t in reads:
            t.readers.append(ins)
        if dma or cc:
            ins.tok = writes[0] if writes else reads[0]
        seen = set()
        best = {}
        for d in deps:
            if d is ins or id(d) in seen:
                continue
            seen.add(id(d))
            if d.dma or d.cc:
                ins.deps.append(d)
                d.sig = True
                continue
            if d.eng == eng and eng == "pe":
                continue
            o = best.get(d.eng)
            if o is None or o.idx < d.idx:
                best[d.eng] = d
        for d in best.values():
            ins.deps.append(d)
            d.sig = True
        self.streams[eng].append(ins)
        return ins

    def emit(self, nc, es):
        eng_sem = {e: es.enter_context(nc.semaphore(f"s_{e}")) for e in ENGS}
        allins = sorted((i for e in ENGS for i in self.streams[e]), key=lambda i: i.idx)
        cnt = {e: 0 for e in ENGS}
        RING = {"sp": 44, "pool": 24, "act": 4, "dve": 2, "pe": 2}
        rings = {}
        dcount = {e: 0 for e in ENGS}
        nsem = 5
        for ins in allins:
            if ins.dma:
                e = ins.eng
                if e not in rings:
                    rings[e] = [es.enter_context(nc.semaphore(f"r_{e}{i}")) for i in range(RING[e])]
                    nsem += RING[e]
                k = dcount[e]
                dcount[e] += 1
                n = RING[e]
                ins.sem = rings[e][k % n]
                ins.val = 16 * (k // n + 1)
                ins.sig = True
                continue
            if not ins.sig:
                continue
            if ins.cc:
                t = ins.tok
                if t.sem is None:
                    t.sem = es.enter_context(nc.semaphore(f"c_{t.name}"))
                    nsem += 1
                t.cnt += 1
                ins.sem = t.sem
                ins.val = t.cnt
            else:
                cnt[ins.eng] += 1
                ins.sem = eng_sem[ins.eng]
                ins.val = cnt[ins.eng]
        self.nsem = nsem
        self.cnt = cnt
        self.dcount = dcount
        block = es.enter_context(nc.Block())

        def run(e, h):
            waited = {}
            for ins in self.streams[e]:
                need = {}
                for d in ins.deps:
                    k = id(d.sem)
                    if k not in need or need[k][1] < d.val:
                        need[k] = (d.sem, d.val)
                for k, (s, v) in need.items():
                    if waited.get(k, 0) >= v:
                        continue
                    h.wait_ge(s, v)
                    waited[k] = v
                if ins.fn is None:
                    continue
                if ins.dma and ins.val > 16 and waited.get(id(ins.sem), 0) < ins.val - 16:
                    h.wait_ge(ins.sem, ins.val - 16)
                    waited[id(ins.sem)] = ins.val - 16
                r = ins.fn(h)
                if ins.sig:
                    r.then_inc(ins.sem, 16 if ins.dma else 1)
            if e in rings:
                n = len(rings[e])
                for i, sm in enumerate(rings[e]):
                    uses = (dcount[e] - i + n - 1) // n if dcount[e] > i else 0
                    if uses > 0 and waited.get(id(sm), 0) < 16 * uses:
                        h.wait_ge(sm, 16 * uses)

        @block.tensor
        def _(h):
            run("pe", h)

        @block.scalar
        def _(h):
            run("act", h)

        @block.vector
        def _(h):
            run("dve", h)

        @block.gpsimd
        def _(h):
            run("pool", h)

        @block.sync
        def _(h):
            run("sp", h)


O_U, O_V, O_ZA, O_QB, O_KB, O_VB, O_ZB, O_CQ, O_CKV, O_KR, O_ZC = (
    0, 1024, 2048, 3072, 4608, 6144, 7680, 9216, 9984, 10496, 10560)


def _inproj_plan():
    perm = []
    blocks = []

    def add(kind, start, n):
        i = 0
        while n > 0:
            w = min(WB, n)
            blocks.append((kind, len(perm), w, i))
            perm.extend(range(start, start + w))
            start += w
            n -= w
            i += 1

    add("cq", O_CQ, 768)
    add("ckv", O_CKV, 512)
    blocks.append(("kr", len(perm), 128, 0))
    perm.extend(range(O_KR, O_KR + 64))
    perm.extend(range(O_KR + 32, O_KR + 64))
    perm.extend(range(O_KR, O_KR + 32))
    add("kb", O_KB, 1536)
    add("vb", O_VB, 1536)
    add("qb", O_QB, 1536)
    add("va", O_V, 1024)
    add("ua", O_U, 1024)
    add("za", O_ZA, 1024)
    add("zb", O_ZB, 1536)
    add("zc", O_ZC, 1536)
    return np.array(perm, dtype=np.int64), blocks


PERM, BLOCKS = _inproj_plan()
DINP = len(PERM)


def _bf(a):
    return np.asarray(a, dtype=np.float32).astype(ml_dtypes.bfloat16)


def build(stop=None, dbg=False, nblk=None, fakecc=False):
    nc = bass.Bass("TRN2", target_bir_lowering=False)
    P = Prog()
    KD = set(os.environ.get("KDBG", "").split(","))

    def din(name, shape, dt):
        return nc.dram_tensor(name, list(shape), dt, kind="ExternalInput").ap()

    def dscr(name, shape, dt):
        return nc.dram_tensor(name, list(shape), dt).ap()

    x_in = din("x", [TL, D], F32)
    pos_in = din("pos", [1, TL], I32)
    invf_in = din("invf", [64, 1], F32)
    maskB_in = din("maskB", [128, 4, 128], BF16)
    maskC_in = din("maskC", [128, 4, 128], BF16)
    maskP_in = din("maskP", [128, 4, 128], BF16)
    tinc_in = din("tinc", [128, 128], BF16)
    ident_in = din("ident", [128, 128], F32)
    triu_in = din("triu", [128, 128], F32)
    gpre_in = din("gpre", [DEPTH, 128, 32], F32)
    gout_in = din("gout", [DEPTH, 128, 32], F32)
    gq_in = din("gq", [DEPTH, 128, 6], F32)
    gkv_in = din("gkv", [DEPTH, 128, 4], F32)
    agv_in = din("agv", [DEPTH, 1, 1024], F32)
    abs_in = din("abs", [DEPTH, 1, 1024], F32)
    wsT_in = din("wsT", [DEPTH, 128, 8, 128], F32)
    gfin_in = din("gfin", [1, D], F32)
    WD = DEPTH if stop is None else 1
    NBLK = len(BLOCKS) if nblk is None else nblk
    WCOLS = BLOCKS[NBLK - 1][1] + BLOCKS[NBLK - 1][2]
    w_in = din("w_in", [WD, D, WCOLS] if (stop is None or stop >= 2) else [1, 128, 128], F32)
    w_uqn = din("w_uqn", [WD, 768, 1536], F32)
    w_uqr = din("w_uqr", [WD, 768, 768], F32)
    w_uqs = din("w_uqs", [WD, 768, 768], F32)
    w_ukk = din("w_ukk", [WD, 512, 1536], F32)
    w_ukv = din("w_ukv", [WD, 512, 1536], F32)
    w_out = din("w_out", [WD, D, D] if (stop is None or stop >= 6) else [1, 128, 128], F32)
    y_out = nc.dram_tensor("y", [TL, D], F32, kind="ExternalOutput").ap()

    xs = [dscr(f"xs{l}", [TL, D], F32) for l in range(DEPTH)]
    QTb = dscr("QTb", [1536, TL], BF16)
    NQTb = dscr("NQTb", [1536, TL], BF16)
    QTcn = dscr("QTcn", [1536, TL], BF16)
    QTcr = dscr("QTcr", [768, TL], BF16)
    uT = dscr("uT", [1024, TL], F32)
    vn = dscr("vn", [TL, 1024], BF16)
    sz = dscr("sz", [4096, TL], F32)
    KTb_i = [dscr(f"KTb_i{k}", [512, TL], BF16) for k in range(3)]
    Vb_i = [dscr(f"Vb_i{k}", [TL, 512], BF16) for k in range(3)]
    KTc_i = [dscr(f"KTc_i{k}", [512, TL], BF16) for k in range(3)]
    KRc_i = dscr("KRc_i", [64, TL], BF16)
    Vc_i = [dscr(f"Vc_i{k}", [TL, 512], BF16) for k in range(3)]
    KTb_g = [dscr(f"KTb_g{k}", [4 * 512, TL], BF16) for k in range(3)]
    Vb_g = [dscr(f"Vb_g{k}", [4 * TL, 512], BF16) for k in range(3)]
    KTc_g = [dscr(f"KTc_g{k}", [4 * 512, TL], BF16) for k in range(3)]
    KRc_g = dscr("KRc_g", [4 * 64, TL], BF16)
    Vc_g = [dscr(f"Vc_g{k}", [4 * TL, 512], BF16) for k in range(3)]

    t_xs = [[Tok(f"xs{l}_{i}") for i in range(NT)] for l in range(DEPTH)]
    t_QTb = [Tok(f"QTb{h}") for h in range(H)]
    t_QTcn = [Tok(f"QTcn{h}") for h in range(H)]
    t_QTcr = [Tok(f"QTcr{h}") for h in range(H)]
    t_uT = [Tok(f"uT{g}") for g in range(8)]
    t_vn = Tok("vn")
    t_sz = [Tok(f"sz{c}") for c in range(32)]
    t_gi = {k: [Tok(f"{k}_i{c}") for c in range(3)] for k in ("KTb", "Vb", "KTc", "Vc")}
    t_gg = {k: [Tok(f"{k}_g{c}") for c in range(3)] for k in ("KTb", "Vb", "KTc", "Vc")}
    t_gi["KRc"] = [Tok("KRc_i")]
    t_gg["KRc"] = [Tok("KRc_g")]
    t_y = Tok("y")

    BASE = 16512
    TOP = 229344
    SLOT = 2048
    nslots = (TOP - BASE + SLOT - 1) // SLOT
    arena = [Tok(f"sb{i}") for i in range(nslots)]

    class Tile:
        def __init__(self, name, shape, dt, off, toks=None):
            nbytes = int(np.prod(shape[1:])) * (4 if dt in (F32, I32) else 2)
            assert off % 32 == 0 and BASE <= off and off + nbytes <= TOP, (name, off, nbytes)
            self.t = nc.alloc_sbuf_tensor_at(name, list(shape), dt, offset=off)
            if toks is None:
                a = (off - BASE) // SLOT
                b = (off + nbytes - 1 - BASE) // SLOT
                toks = arena[a:b + 1]
            self.toks = list(toks)
            self.end = off + nbytes

        def __getitem__(self, k):
            return self.t[k]

    o = BASE
    hT = Tile("hT", [128, 32, TL], BF16, o, toks=[Tok("hT")]); o = hT.end
    wsl = []
    for i in range(2):
        wsl.append(Tile(f"wsl{i}", [128, 8192], BF16, o, toks=[Tok(f"wsl{i}")])); o = wsl[-1].end
    R3 = o

    ctop = TOP
    def ctile(name, shape, dt):
        nonlocal ctop
        nbytes = int(np.prod(shape[1:])) * (4 if dt in (F32, I32) else 2)
        nbytes = (nbytes + 31) // 32 * 32
        ctop -= nbytes
        return Tile(name, shape, dt, ctop, toks=[Tok(name)])

    ones_bf = ctile("ones_bf", [128, 128], BF16)
    ones_f = ctile("ones_f", [128, 128], F32)
    ident = ctile("ident", [128, 128], F32)
    tinc = ctile("tinc", [128, 128], BF16)
    e0 = ctile("e0", [128, 1], F32)
    maskB = ctile("maskB", [128, 4, 128], BF16)
    maskC = ctile("maskC", [128, 4, 128], BF16)
    maskP = ctile("maskP", [128, 4, 128], BF16)
    ident_bf = ctile("ident_bf", [128, 128], BF16)
    triu = ctile("triu", [128, 128], F32)
    gpre = ctile("gpre", [128, DEPTH, 32], F32)
    gout = ctile("gout", [128, DEPTH, 32], F32)
    gq = ctile("gq", [128, DEPTH, 6], F32)
    gkv = ctile("gkv", [128, DEPTH, 4], F32)
    invf = ctile("invf", [64, 1], F32)
    ssqc = ctile("ssqc", [128, 16], F32)
    rstc = ctile("rstc", [128, 16], F32)
    rkvc = ctile("rkvc", [128, 8], F32)
    rmc = [ctile(f"rmc{m}", [128, 8], F32) for m in range(3)]
    cos2 = ctile("cos2", [64, TL], F32)
    sins = ctile("sins", [64, TL], F32)
    R3END = ctop // 32 * 32

    def r3(name, shape, dt, off):
        t = Tile(name, shape, dt, R3 + off)
        assert t.end <= R3END, (name, t.end, R3END)
        return t

    xt = [r3(f"xt{i}", [128, D], F32, i * 16384) for i in range(2)]
    cqT = r3("cqT", [128, 6, TL], BF16, 32768)
    ckvT = r3("ckvT", [128, 4, TL], BF16, 45056)
    rq_b = r3("rq_b", [128, TL], F32, 53248)
    rkv_b = r3("rkv_b", [128, TL], F32, 57344)
    CR = r3("CR", [64, TL], F32, 61440)
    SR = r3("SR", [64, TL], F32, 65536)
    stf = [r3(f"stf{i}", [128, 512], F32, 69632 + 2048 * i) for i in range(3)]
    stb = [r3(f"stb{i}", [128, 512], BF16, 75776 + 1024 * i) for i in range(4)]
    tmf = [r3(f"tmf{i}", [128, 512], F32, 79872 + 2048 * i) for i in range(3)]
    agv_b = r3("agv_b", [128, 1024], F32, 86016)
    junk = r3("junk", [128, D], BF16, 90112)
    sc = [r3(f"sc{i}", [64, TL], F32, 16384 + 4096 * i) for i in range(4)]
    kbuf = [r3(f"kbuf{i}", [128, 4, TL], BF16, 0 + 8192 * i) for i in range(2)]
    vbuf = [r3(f"vbuf{i}", [128, 32, 128], BF16, 16384 + 8192 * i) for i in range(2)]
    qbuf = [r3(f"qbuf{i}", [128, 2, TL], BF16, 32768 + 4096 * i) for i in range(2)]
    krbuf = r3("krbuf", [64, 4, TL], BF16, 40960)
    e_f = [r3(f"e_f{i}", [128, 512], F32, 49152 + 2048 * i) for i in range(2)]
    sp_b = [r3(f"sp_b{i}", [128, 512], BF16, 53248 + 1024 * i) for i in range(2)]
    a_b = [r3(f"a_b{i}", [128, 512], BF16, 55296 + 1024 * i) for i in range(2)]
    tm_f = [r3(f"tm_f{i}", [128, 512], F32, 61440 + 2048 * i) for i in range(2)]
    tm3 = tm_f + [r3("tm_f2", [128, 512], F32, 57344)]
    carry = [r3(f"carry{i}", [128, 512], F32, 65536 + 2048 * i) for i in range(2)]
    sqacc = r3("sqacc", [128, TL], F32, 69632)
    szt = [r3(f"szt{i}", [128, TL], F32, 73728 + 4096 * i) for i in range(2)]
    ysb = [r3(f"ysb{i}", [128, TL], F32, 81920 + 4096 * i) for i in range(2)]
    sqt = [r3(f"sqt{i}", [128, 512], F32, 90112 + 2048 * i) for i in range(2)]
    vnt = r3("vnt", [128, NT, 1024], BF16, 0)
    wsT = r3("wsT", [128, 8, 128], BF16, 16384)
    wsTf = r3("wsTf", [128, 8, 128], F32, 20480)
    bsb = r3("bsb", [128, 8, 128], F32, 24576)
    xres = [r3(f"xres{i}", [128, NT, WB], F32, 0 + 8192 * i) for i in range(2)]
    onew = [r3(f"onew{i}", [128, WB], F32, 16384 + 1024 * i) for i in range(2)]
    rm_b = r3("rm_b", [128, TL], F32, 94208)
    gfin_b = r3("gfin_b", [128, D], F32, 32768)

    ps = [nc.alloc_psum_tensor(f"ps{b}", [128, 512], F32) for b in range(8)]
    t_ps = [Tok(f"ps{b}", excl=True) for b in range(8)]

    rr = {}

    def rot(key, n):
        v = rr.get(key, 0)
        rr[key] = v + 1
        return v % n

    def dma(eng, out, in_, reads, writes, partial=False):
        P.op(eng, lambda h: h.dma_start(out=out, in_=in_), reads=reads, writes=writes, partial=partial, dma=True)

    def mm(out, lhsT, rhs, start, stop, reads, writes, sgc=False):
        P.op("pe", lambda h: h.matmul(out, lhsT, rhs, start=start, stop=stop, skip_group_check=sgc), reads=reads, writes=writes, partial=True)

    def act(out, in_, func, reads, writes, scale=None, bias=None, accum_out=None, partial=False):
        kw = {}
        if scale is not None:
            kw["scale"] = scale
        if bias is not None:
            kw["bias"] = bias
        if accum_out is not None:
            kw["accum_out"] = accum_out
        P.op("act", lambda h: h.activation(out=out, in_=in_, func=func, **kw), reads=reads, writes=writes, partial=partial)

    def tt(eng, out, in0, in1, op, reads, writes, partial=False):
        P.op(eng, lambda h: h.tensor_tensor(out=out, in0=in0, in1=in1, op=op), reads=reads, writes=writes, partial=partial)

    def ts(eng, out, in0, s1, op0, reads, writes, s2=None, op1=None, partial=False):
        if op1 is None:
            P.op(eng, lambda h: h.tensor_scalar(out=out, in0=in0, scalar1=s1, scalar2=None, op0=op0), reads=reads, writes=writes, partial=partial)
        else:
            P.op(eng, lambda h: h.tensor_scalar(out=out, in0=in0, scalar1=s1, scalar2=s2, op0=op0, op1=op1), reads=reads, writes=writes, partial=partial)

    def stt(out, in0, scalar, in1, op0, op1, reads, writes, partial=False):
        P.op("dve", lambda h: h.scalar_tensor_tensor(out=out, in0=in0, scalar=scalar, in1=in1, op0=op0, op1=op1),
             reads=reads, writes=writes, partial=partial)

    def recip(out, in_, reads, writes):
        P.op("dve", lambda h: h.reciprocal(out=out, in_=in_), reads=reads, writes=writes)

    def memset(eng, ap, val, writes):
        P.op(eng, lambda h: h.memset(ap, val), writes=writes)

    def copy(eng, out, in_, reads, writes, partial=False):
        P.op(eng, lambda h: h.tensor_copy(out=out, in_=in_), reads=reads, writes=writes, partial=partial)

    def wload(src3):
        kc, n = src3.shape[1], src3.shape[2]
        assert kc * n <= 8192
        s = rot("wsl", 2)
        t = wsl[s]
        view = t.t[:, 0:kc * n].rearrange("p (k n) -> p k n", n=n)
        hk = max(1, kc // 2)
        first = True
        for k0 in range(0, kc, hk):
            k1 = min(kc, k0 + hk)
            dma("pool", view[:, k0:k1, :], src3[:, k0:k1, :], reads=[], writes=t.toks, partial=not first)
            first = False
        return view, t

    dma("sp", ident[:], ident_in, [], ident.toks)
    dma("sp", tinc[:], tinc_in, [], tinc.toks)
    dma("sp", maskB[:], maskB_in, [], maskB.toks)
    dma("sp", maskC[:], maskC_in, [], maskC.toks)
    dma("sp", maskP[:], maskP_in, [], maskP.toks)
    copy("dve", ident_bf[:], ident[:], ident.toks, ident_bf.toks)
    dma("sp", triu[:], triu_in, [], triu.toks)
    dma("sp", invf[:], invf_in, [], invf.toks)
    for l in range(DEPTH):
        dma("sp", gpre[:, l, :], gpre_in[l], [], gpre.toks, partial=(l > 0))
        dma("sp", gout[:, l, :], gout_in[l], [], gout.toks, partial=(l > 0))
        dma("sp", gq[:, l, :], gq_in[l], [], gq.toks, partial=(l > 0))
        dma("sp", gkv[:, l, :], gkv_in[l], [], gkv.toks, partial=(l > 0))
    memset("dve", ones_bf[:], 1.0, ones_bf.toks)
    memset("dve", ones_f[:], 1.0, ones_f.toks)
    memset("dve", e0[:], 0.0, e0.toks)
    P.op("dve", lambda h: h.memset(e0[0:1, :], 1.0), writes=e0.toks)

    posi_t = Tile("posi", [64, TL], I32, R3 + 0)
    dma("sp", posi_t[:], pos_in[0].partition_broadcast(64), [], posi_t.toks)
    ang, kf, rr_, mm_ = sc
    copy("dve", ang[:], posi_t[:], posi_t.toks, ang.toks)
    ts("dve", ang[:], ang[:], invf[:, 0:1], ALU.mult, ang.toks + invf.toks, ang.toks)
    ts("dve", kf[:], ang[:], float(1.0 / (2 * np.pi)), ALU.mult, ang.toks, kf.toks)
    ki = Tile("ki", [64, TL], I32, R3 + 4096)
    copy("dve", ki[:], kf[:], kf.toks, ki.toks)
    copy("dve", kf[:], ki[:], ki.toks, kf.toks)
    C1 = 6.28125
    C2 = float(np.float32(2 * np.pi - C1))
    C3 = float(2 * np.pi - C1 - C2)
    stt(rr_[:], kf[:], -C1, ang[:], ALU.mult, ALU.add, kf.toks + ang.toks, rr_.toks)
    stt(rr_[:], kf[:], -C2, rr_[:], ALU.mult, ALU.add, kf.toks + rr_.toks, rr_.toks)
    stt(rr_[:], kf[:], -C3, rr_[:], ALU.mult, ALU.add, kf.toks + rr_.toks, rr_.toks)
    PI = float(np.pi)

    def wrap(t):
        ts("dve", mm_[:], t[:], PI, ALU.is_gt, t.toks, mm_.toks, s2=-2 * PI, op1=ALU.mult)
        tt("dve", t[:], t[:], mm_[:], ALU.add, t.toks + mm_.toks, t.toks)
        ts("dve", mm_[:], t[:], -PI, ALU.is_lt, t.toks, mm_.toks, s2=2 * PI, op1=ALU.mult)
        tt("dve", t[:], t[:], mm_[:], ALU.add, t.toks + mm_.toks, t.toks)

    wrap(rr_)
    act(sins[:], rr_[:], AF.Sin, rr_.toks, sins.toks)
    ts("dve", rr_[:], rr_[:], PI / 2, ALU.add, rr_.toks, rr_.toks)
    wrap(rr_)
    act(cos2[:], rr_[:], AF.Sin, rr_.toks, cos2.toks)
    ts("dve", sins[0:32, :], sins[0:32, :], -1.0, ALU.mult, sins.toks, sins.toks)

    def rmsnorm_stats(xt_t, col, width, eng_sq="act"):
        act(junk[:, 0:width], xt_t, AF.Square, xt_t_toks[0], junk.toks + ssqc.toks, accum_out=ssqc[:, col:col + 1])
        ts("dve", rstc[:, col:col + 1], ssqc[:, col:col + 1], 1.0 / width, ALU.mult, ssqc.toks, rstc.toks, s2=EPS, op1=ALU.add)
        act(rstc[:, col:col + 1], rstc[:, col:col + 1], AF.Sqrt, rstc.toks, rstc.toks)
        recip(rstc[:, col:col + 1], rstc[:, col:col + 1], rstc.toks, rstc.toks)

    xt_t_toks = [None]

    def phase_A1(l, xsrc, xsrc_toks):
        for i in range(NT):
            s = i % 2
            X = xt[s]
            dma("sp", X[:], xsrc[i * 128:(i + 1) * 128, :], [xsrc_toks[i]] if xsrc_toks else [], X.toks)
            xt_t_toks[0] = X.toks
            rmsnorm_stats(X[:], i, D)
            ts("dve", X[:], X[:], rstc[:, i:i + 1], ALU.mult, X.toks + rstc.toks, X.toks)
            for j in range(8):
                b = 2 + (j % 2)
                for q in range(4):
                    kc = 4 * j + q
                    P.op("pe", lambda h, b=b, q=q, kc=kc, X=X: h.transpose(out=ps[b][:, q * 128:(q + 1) * 128],
                                                                     in_=X[:, kc * 128:(kc + 1) * 128], identity=ident[:]),
                         reads=X.toks + ident.toks, writes=[t_ps[b]], partial=True)
                for q in range(4):
                    kc = 4 * j + q
                    src = ps[b][:, q * 128:(q + 1) * 128]
                    dst = hT[:, kc, i * 128:(i + 1) * 128]
                    ts("dve", dst, src, gpre[:, l, kc:kc + 1], ALU.mult, [t_ps[b]] + gpre.toks, hT.toks, partial=True)

    def acc_bank():
        return rot("acc", 4)

    def st_f():
        return stf[rot("stf", 3)]

    def st_b():
        return stb[rot("stb", 4)]

    def tm():
        return tmf[rot("tmf", 3)]

    def bcast_rstd(sqb0, width, dst):
        for th in range(2):
            cs = slice(th * 512, (th + 1) * 512)
            t1 = tm()
            act(t1[:], ps[sqb0 + th][:], AF.Sqrt, [t_ps[sqb0 + th]], t1.toks, scale=1.0 / width, bias=EPS)
            recip(dst[:, cs], t1[:], t1.toks, dst.toks)

    def tokmajor_cols(src_b, dstc):
        for i in range(NT):
            mm(ps[7][:, i:i + 1], src_b[:, i * 128:(i + 1) * 128], e0[:], True, True, src_b.toks + e0.toks, [t_ps[7]])
        copy("dve", dstc[:, 0:NT], ps[7][:, 0:NT], [t_ps[7]], dstc.toks)

    def phase_A2_full(l):
        wv_all = w_in[l].rearrange("(kc p) n -> p kc n", p=128)
        dma("sp", agv_b[:], agv_in[l, 0].partition_broadcast(128), [], agv_b.toks)
        for (kind, c0, ncols, bi) in BLOCKS[:NBLK]:
            if kind == "ckv" and bi == 0:
                bcast_rstd(4, 768.0, rq_b)
                tt("dve", CR[:], rq_b[0:64, :], cos2[:], ALU.mult, rq_b.toks + cos2.toks, CR.toks)
                tt("dve", SR[:], rq_b[0:64, :], sins[:], ALU.mult, rq_b.toks + sins.toks, SR.toks)
            if kind == "kr":
                bcast_rstd(4, 512.0, rkv_b)
                tokmajor_cols(rkv_b, rkvc)
            if kind == "kb" and bi == 0:
                mla_upproj(l)
            _blk(l, wv_all, kind, c0, ncols, bi)
            if kind == "vb" and bi == 5:
                gathers()

    def _blk(l, wv_all, kind, c0, ncols, bi):
        wv, wt = wload(wv_all[:, :, c0:c0 + ncols])
        if kind in ("vb", "va"):
            for i in range(NT):
                b = acc_bank()
                for kc in range(32):
                    mm(ps[b][:, 0:ncols], hT[:, kc, i * 128:(i + 1) * 128], wv[:, kc, :], kc == 0, kc == 31,
                       hT.toks + wt.toks, [t_ps[b]])
                if kind == "vb":
                    sb = st_b()
                    if i % 2 == 0:
                        act(sb[:, 0:ncols], ps[b][:, 0:ncols], AF.Copy, [t_ps[b]], sb.toks)
                    else:
                        copy("dve", sb[:, 0:ncols], ps[b][:, 0:ncols], [t_ps[b]], sb.toks)
                    vc0 = bi * WB
                    dma("sp", Vb_i[vc0 // 512][i * 128:(i + 1) * 128, vc0 % 512:vc0 % 512 + ncols], sb[:, 0:ncols], sb.toks, [t_gi["Vb"][vc0 // 512]], partial=True)
                else:
                    gt = tm()
                    act(gt[:, 0:ncols], ps[b][:, 0:ncols], AF.Gelu_apprx_tanh, [t_ps[b]], gt.toks)
                    sb = st_b()
                    for g2 in range(ncols // 128):
                        col = 8 + g2
                        sl = slice(g2 * 128, (g2 + 1) * 128)
                        act(junk[:, sl], gt[:, sl], AF.Square, gt.toks, junk.toks + ssqc.toks, accum_out=ssqc[:, col:col + 1])
                        ts("dve", rstc[:, col:col + 1], ssqc[:, col:col + 1], 1.0 / 128, ALU.mult, ssqc.toks, rstc.toks, s2=EPS, op1=ALU.add)
                        act(rstc[:, col:col + 1], rstc[:, col:col + 1], AF.Sqrt, rstc.toks, rstc.toks)
                        recip(rstc[:, col:col + 1], rstc[:, col:col + 1], rstc.toks, rstc.toks)
                        gcol = bi * WB + g2 * 128
                        stt(sb[:, sl], gt[:, sl], rstc[:, col:col + 1], agv_b[:, gcol:gcol + 128], ALU.mult, ALU.mult,
                            gt.toks + rstc.toks + agv_b.toks, sb.toks, partial=(g2 > 0))
                    dma("sp", vn[i * 128:(i + 1) * 128, bi * WB:bi * WB + ncols], sb[:, 0:ncols], sb.toks, [t_vn], partial=True)
            return
        if kind == "kr":
            for th in range(2):
                cs = slice(th * 512, (th + 1) * 512)
                bA = acc_bank()
                for kc in range(32):
                    mm(ps[bA][0:64, :], wv[:, kc, 0:64], hT[:, kc, cs], kc == 0, kc == 31, hT.toks + wt.toks, [t_ps[bA]])
                bB = acc_bank()
                for kc in range(32):
                    mm(ps[bB][0:64, :], wv[:, kc, 64:128], hT[:, kc, cs], kc == 0, kc == 31, hT.toks + wt.toks, [t_ps[bB]])
                t1 = tm(); t2 = tm(); sb = st_b()
                tt("dve", t1[0:64, :], ps[bA][0:64, :], cos2[:, cs], ALU.mult, [t_ps[bA]] + cos2.toks, t1.toks)
                tt("dve", t2[0:64, :], ps[bB][0:64, :], sins[:, cs], ALU.mult, [t_ps[bB]] + sins.toks, t2.toks)
                tt("dve", sb[0:64, :], t1[0:64, :], t2[0:64, :], ALU.add, t1.toks + t2.toks, sb.toks)
                dma("sp", KRc_i[:, cs], sb[0:64, :], sb.toks, [t_gi["KRc"][0]], partial=True)
            return
        for m in range(ncols // 128):
            fcol = bi * WB + m * 128
            fch = fcol // 128
            for th in range(2):
                cs = slice(th * 512, (th + 1) * 512)
                b = acc_bank()
                for kc in range(32):
                    mm(ps[b][:], wv[:, kc, m * 128:(m + 1) * 128], hT[:, kc, cs], kc == 0, kc == 31, hT.toks + wt.toks, [t_ps[b]])
                src = ps[b][:]
                if kind in ("cq", "ckv"):
                    dstT = cqT if kind == "cq" else ckvT
                    gg = gq if kind == "cq" else gkv
                    sqb = 4 + th
                    nch = 6 if kind == "cq" else 4
                    sq = tm()
                    if "nossq" not in KD:
                        act(sq[:], src, AF.Square, [t_ps[b]], sq.toks)
                        if "nomm" not in KD:
                            mm(ps[sqb][:], ones_f[:], sq[:], fch == 0, fch == nch - 1, ones_f.toks + sq.toks, [t_ps[sqb]])
                    if "nocqT" not in KD:
                        ts("dve", dstT[:, fch, cs], src, gg[:, l, fch:fch + 1], ALU.mult, [t_ps[b]] + gg.toks, dstT.toks, partial=True)
                    else:
                        copy("dve", dstT[:, fch, cs], src, [t_ps[b]], dstT.toks, partial=True)
                elif kind == "kb":
                    sb = st_b()
                    if th == 0:
                        act(sb[:], src, AF.Copy, [t_ps[b]], sb.toks)
                    else:
                        copy("dve", sb[:], src, [t_ps[b]], sb.toks)
                    dma("sp", KTb_i[fch // 4][(fch % 4) * 128:(fch % 4 + 1) * 128, cs], sb[:], sb.toks, [t_gi["KTb"][fch // 4]], partial=True)
                elif kind == "qb":
                    sb = st_b(); sb2 = st_b()
                    act(sb[:], src, AF.Copy, [t_ps[b]], sb.toks)
                    act(sb2[:], src, AF.Copy, [t_ps[b]], sb2.toks, scale=-SCALE_B)
                    dma("sp", QTb[fcol:fcol + 128, cs], sb[:], sb.toks, [t_QTb[fch]], partial=True)
                    dma("sp", NQTb[fcol:fcol + 128, cs], sb2[:], sb2.toks, [t_QTb[fch]], partial=True)
                elif kind == "ua":
                    sf = st_f()
                    act(sf[:], src, AF.Gelu_apprx_tanh, [t_ps[b]], sf.toks)
                    dma("sp", uT[fcol:fcol + 128, cs], sf[:], sf.toks, [t_uT[fch]], partial=True)
                else:
                    base = {"za": 0, "zb": 1024, "zc": 2560}[kind]
                    sf = st_f()
                    act(sf[:], src, AF.Silu, [t_ps[b]], sf.toks)
                    r0 = base + fcol
                    dma("sp", sz[r0:r0 + 128, cs], sf[:], sf.toks, [t_sz[r0 // 128]], partial=True)

    def mla_upproj(l):
        for blk in range(2):
            wv, wt = wload(w_uqn[l].rearrange("(kc p) n -> p kc n", p=128)[:, :, blk * 768:(blk + 1) * 768])
            for hh in range(6):
                h_ = blk * 6 + hh
                for th in range(2):
                    cs = slice(th * 512, (th + 1) * 512)
                    b = acc_bank()
                    for kc in range(6):
                        mm(ps[b][:], wv[:, kc, hh * 128:(hh + 1) * 128], cqT[:, kc, cs], kc == 0, kc == 5, cqT.toks + wt.toks, [t_ps[b]])
                    sb = st_b()
                    tt("dve", sb[:], ps[b][:], rq_b[:, cs], ALU.mult, [t_ps[b]] + rq_b.toks, sb.toks)
                    dma("sp", QTcn[h_ * 128:(h_ + 1) * 128, cs], sb[:], sb.toks, [t_QTcn[h_]], partial=True)
        wr, wrt = wload(w_uqr[l].rearrange("(kc p) n -> p kc n", p=128))
        wsw, wswt = wload(w_uqs[l].rearrange("(kc p) n -> p kc n", p=128))
        for h_ in range(H):
            for th in range(2):
                cs = slice(th * 512, (th + 1) * 512)
                bA = acc_bank()
                for kc in range(6):
                    mm(ps[bA][0:64, :], wr[:, kc, h_ * 64:(h_ + 1) * 64], cqT[:, kc, cs], kc == 0, kc == 5, cqT.toks + wrt.toks, [t_ps[bA]])
                bB = acc_bank()
                for kc in range(6):
                    mm(ps[bB][0:64, :], wsw[:, kc, h_ * 64:(h_ + 1) * 64], cqT[:, kc, cs], kc == 0, kc == 5, cqT.toks + wswt.toks, [t_ps[bB]])
                t1 = tm(); t2 = tm(); sb = st_b()
                tt("dve", t1[0:64, :], ps[bA][0:64, :], CR[:, cs], ALU.mult, [t_ps[bA]] + CR.toks, t1.toks)
                tt("dve", t2[0:64, :], ps[bB][0:64, :], SR[:, cs], ALU.mult, [t_ps[bB]] + SR.toks, t2.toks)
                tt("dve", sb[0:64, :], t1[0:64, :], t2[0:64, :], ALU.add, t1.toks + t2.toks, sb.toks)
                dma("sp", QTcr[h_ * 64:(h_ + 1) * 64, cs], sb[0:64, :], sb.toks, [t_QTcr[h_]], partial=True)
        for blk in range(2):
            wv, wt = wload(w_ukk[l].rearrange("(kc p) n -> p kc n", p=128)[:, :, blk * 768:(blk + 1) * 768])
            for hh in range(6):
                h_ = blk * 6 + hh
                for th in range(2):
                    cs = slice(th * 512, (th + 1) * 512)
                    b = acc_bank()
                    for kc in range(4):
                        mm(ps[b][:], wv[:, kc, hh * 128:(hh + 1) * 128], ckvT[:, kc, cs], kc == 0, kc == 3, ckvT.toks + wt.toks, [t_ps[b]])
                    sb = st_b()
                    tt("dve", sb[:], ps[b][:], rkv_b[:, cs], ALU.mult, [t_ps[b]] + rkv_b.toks, sb.toks)
                    dma("sp", KTc_i[h_ // 4][(h_ % 4) * 128:(h_ % 4 + 1) * 128, cs], sb[:], sb.toks, [t_gi["KTc"][h_ // 4]], partial=True)
        for blk in range(2):
            wv, wt = wload(w_ukv[l].rearrange("(kc p) n -> p kc n", p=128)[:, :, blk * 768:(blk + 1) * 768])
            for i in range(NT):
                for n3 in range(3):
                    b = acc_bank()
                    for kc in range(4):
                        mm(ps[b][:, 0:256], ckvT[:, kc, i * 128:(i + 1) * 128], wv[:, kc, n3 * 256:(n3 + 1) * 256], kc == 0, kc == 3,
                           ckvT.toks + wt.toks, [t_ps[b]])
                    sb = st_b()
                    ts("dve", sb[:, 0:256], ps[b][:, 0:256], rkvc[:, i:i + 1], ALU.mult, [t_ps[b]] + rkvc.toks, sb.toks)
                    c0 = blk * 768 + n3 * 256
                    dma("sp", Vc_i[c0 // 512][i * 128:(i + 1) * 128, c0 % 512:c0 % 512 + 256], sb[:, 0:256], sb.toks, [t_gi["Vc"][c0 // 512]], partial=True)

    RG = [[0, 1, 2, 3], [4, 5, 6, 7]]

    def gathers():
        lst = []
        for k, src, dst in (("KTb", KTb_i, KTb_g), ("Vb", Vb_i, Vb_g), ("KTc", KTc_i, KTc_g), ("Vc", Vc_i, Vc_g)):
            for c in range(3):
                lst.append((t_gi[k][c], t_gg[k][c], src[c], dst[c]))
        lst.insert(6, (t_gi["KRc"][0], t_gg["KRc"][0], KRc_i, KRc_g))
        for ti, tg, src, dst in lst:
            if fakecc:
                rows = src.shape[0]
                for j in range(4):
                    dma("sp", dst[j * rows:(j + 1) * rows, :], src, [ti], [tg], partial=(j > 0))
                continue
            P.op("pool", lambda h, src=src, dst=dst: h.collective_compute("AllGather", ALU.bypass, replica_groups=RG, ins=[src], outs=[dst], dma_qos="P3"),
                 reads=[ti], writes=[tg], cc=True)

    def load_sz(fc):
        Z = szt[rot("szt", 2)]
        dma("sp", Z[:], sz[fc * 128:(fc + 1) * 128, :], [t_sz[fc]], Z.toks)
        return Z

    def gate_out(l, fc, srcs, src_toks, first_of_mixer, Z):
        for th in range(2):
            cs = slice(th * 512, (th + 1) * 512)
            stt(hT[:, fc, cs], srcs[th], gout[:, l, fc:fc + 1], Z[:, cs], ALU.mult, ALU.mult,
                src_toks[th] + gout.toks + Z.toks, hT.toks, partial=True)
            if first_of_mixer:
                act(sqacc[:, cs], srcs[th], AF.Square, src_toks[th], sqacc.toks, partial=(th > 0))
            else:
                q = sqt[rot("sqt", 2)]
                act(q[:], srcs[th], AF.Square, src_toks[th], q.toks)
                tt("pool", sqacc[:, cs], sqacc[:, cs], q[:], ALU.add, sqacc.toks + q.toks, sqacc.toks)

    def mixer_rstd(m, width):
        for th in range(2):
            cs = slice(th * 512, (th + 1) * 512)
            mm(ps[7][:], ones_f[:], sqacc[:, cs], True, True, ones_f.toks + sqacc.toks, [t_ps[7]])
            t1 = tm_f[rot("tm_f", 2)]
            act(t1[:], ps[7][:], AF.Sqrt, [t_ps[7]], t1.toks, scale=1.0 / width, bias=EPS)
            recip(rm_b[:, cs], t1[:], t1.toks, rm_b.toks)
        tokmajor_cols(rm_b, rmc[m])

    def mixer_A(l):
        dma("sp", vnt[:], vn.rearrange("(i p) f -> p i f", p=128), [t_vn], vnt.toks)
        dma("sp", wsTf[:], wsT_in[l], [], wsTf.toks)
        dma("sp", bsb[:], abs_in[l, 0].partition_broadcast(128).rearrange("p (g t) -> p g t", g=8), [], bsb.toks)
        for g in range(8):
            tt("dve", wsT[:, g, :], wsTf[:, g, :], triu[:], ALU.mult, wsTf.toks + triu.toks, wsT.toks, partial=(g > 0))
        for g in range(8):
            U = ysb[rot("ysb", 2)]
            dma("sp", U[:], uT[g * 128:(g + 1) * 128, :], [t_uT[g]], U.toks)
            Y = ysb[rot("ysb", 2)]
            for th in range(2):
                b = acc_bank()
                for q in range(4):
                    i = th * 4 + q
                    mm(ps[b][:, q * 128:(q + 1) * 128], vnt[:, i, g * 128:(g + 1) * 128], wsT[:, g, :], True, True,
                       vnt.toks + wsT.toks, [t_ps[b]])
                cs = slice(th * 512, (th + 1) * 512)
                t1 = tm_f[rot("tm_f", 2)]
                tt("dve", t1[:].rearrange("p (q t) -> p q t", q=4), ps[b][:].rearrange("p (q t) -> p q t", q=4),
                   bsb[:, g, :].unsqueeze(1).broadcast_to([128, 4, 128]), ALU.add, [t_ps[b]] + bsb.toks, t1.toks)
                tt("pool", Y[:, cs], t1[:], U[:, cs], ALU.mult, t1.toks + U.toks, Y.toks, partial=(th > 0))
            gate_out(l, g, [Y[:, 0:512], Y[:, 512:1024]], [Y.toks, Y.toks], g == 0, load_sz(g))
        mixer_rstd(0, 1024.0)

    def head_steps():
        st = []
        for c in range(2):
            ko = [(G, j) for G in range(4 * c + 3, -1, -1) for j in range(3, -1, -1)]
            for n_, (G, j) in enumerate(ko):
                st.append((c, G, j, n_ == 0, n_ == len(ko) - 1))
        return st

    STEPS = head_steps()

    def geom(c, G):
        ilo = max(G, 4 * c)
        c0 = (ilo - 4 * c) * 128
        return c0, 512 - c0, slice(c * 512 + c0, (c + 1) * 512), G >= 4 * c

    def attn_loads(h_, mixer):
        s = h_ % 2
        K = kbuf[s]; V = vbuf[s]; Q = qbuf[s]
        KT_g, V_g, kk, vk = (KTb_g, Vb_g, "KTb", "Vb") if mixer == "B" else (KTc_g, Vc_g, "KTc", "Vc")
        ch, hl = h_ // 4, h_ % 4
        dma("sp", K[:], KT_g[ch].rearrange("(j r) c -> r j c", j=4)[hl * 128:(hl + 1) * 128], [t_gg[kk][ch]], K.toks)
        Vv = V_g[ch].rearrange("(j g p) c -> p j g c", j=4, g=8)
        for j in range(4):
            dma("sp", V[:, j * 8:(j + 1) * 8, :], Vv[:, j, :, hl * 128:(hl + 1) * 128], [t_gg[vk][ch]], V.toks, partial=(j > 0))
        if mixer == "B":
            dma("sp", Q[:, 0, :], QTb[h_ * 128:(h_ + 1) * 128, :], [t_QTb[h_]], Q.toks)
            dma("sp", Q[:, 1, :], NQTb[h_ * 128:(h_ + 1) * 128, :], [t_QTb[h_]], Q.toks, partial=True)
        else:
            dma("sp", Q[:, 0, :], QTcn[h_ * 128:(h_ + 1) * 128, :], [t_QTcn[h_]], Q.toks)
            dma("sp", Q[0:64, 1, :], QTcr[h_ * 64:(h_ + 1) * 64, :], [t_QTcr[h_]], Q.toks, partial=True)
        return K, V, Q

    def mixer_B(l):
        nxt = attn_loads(0, "B")
        for h_ in range(H):
            K, V, Q = nxt
            Z = load_sz(8 + h_)
            if h_ + 1 < H:
                nxt = attn_loads(h_ + 1, "B")
            for c in range(2):
                memset("pool", carry[c][:], 0.0, carry[c].toks)
            n = len(STEPS)
            st = {}
            for t in range(n + 3):
                if t < n:
                    c, G, j, first, last = STEPS[t]
                    c0, N, qs, masked = geom(c, G)
                    sb_ = rot("Sb", 2)
                    mm(ps[sb_][:, 0:N], K[:, j, G * 128:(G + 1) * 128], Q[:, 0, qs], True, not masked, K.toks + Q.toks, [t_ps[sb_]], sgc=True)
                    if masked:
                        mm(ps[sb_][:, 0:128], ident_bf[:], maskB[:, j, :], False, True, ident_bf.toks + maskB.toks, [t_ps[sb_]], sgc=True)
                    E = e_f[rot("e_f", 2)]
                    act(E[:, 0:N], ps[sb_][:, 0:N], AF.Exp, [t_ps[sb_]], E.toks, scale=SCALE_B)
                    SP = sp_b[rot("sp_b", 2)]
                    act(SP[:, 0:N], E[:, 0:N], AF.Ln, E.toks, SP.toks, bias=1.0)
                    st[t] = [SP, None]
                if 1 <= t <= n:
                    c, G, j, first, last = STEPS[t - 1]
                    c0, N, qs, masked = geom(c, G)
                    SP = st[t - 1][0]
                    cb = 2 + rot("Cb", 2)
                    kt_ap = K[:, j, G * 128:(G + 1) * 128]
                    mm(ps[cb][:, 0:N], tinc[:], SP[:, 0:N], True, False, tinc.toks + SP.toks, [t_ps[cb]], sgc=True)
                    mm(ps[cb][:, 0:N], kt_ap, Q[:, 1, qs], False, not masked, K.toks + Q.toks, [t_ps[cb]], sgc=True)
                    if masked:
                        mm(ps[cb][:, 0:128], ident_bf[:], maskP[:, j, :], False, True, ident_bf.toks + maskP.toks, [t_ps[cb]], sgc=True)
                    mm(ps[4][:, 0:N], ones_bf[:], SP[:, 0:N], True, True, ones_bf.toks + SP.toks, [t_ps[4]])
                    T1 = tm3[rot("tm3", 3)]
                    CY = carry[c]
                    tt("dve", T1[:, 0:N], ps[cb][:, 0:N], CY[:, c0:512], ALU.add, [t_ps[cb]] + CY.toks, T1.toks)
                    tt("dve", CY[:, c0:512], ps[4][:, 0:N], CY[:, c0:512], ALU.add, [t_ps[4]] + CY.toks, CY.toks)
                    st[t - 1].append(T1)
                if 2 <= t <= n + 1:
                    c, G, j, first, last = STEPS[t - 2]
                    c0, N, qs, masked = geom(c, G)
                    T1 = st[t - 2][2]
                    A_ = a_b[rot("a_b", 2)]
                    act(A_[:, 0:N], T1[:, 0:N], AF.Exp, T1.toks, A_.toks, scale=-1.0)
                    st[t - 2][1] = A_
                if t >= 3:
                    c, G, j, first, last = STEPS[t - 3]
                    c0, N, qs, masked = geom(c, G)
                    A_ = st[t - 3][1]
                    mm(ps[5 + c][:, c0:512], V[:, j * 8 + G, :], A_[:, 0:N], first, last, V.toks + A_.toks, [t_ps[5 + c]], sgc=True)
            gate_out(l, 8 + h_, [ps[5][:], ps[6][:]], [[t_ps[5]], [t_ps[6]]], h_ == 0, Z)
        mixer_rstd(1, 1536.0)

    def mixer_C(l):
        dma("sp", krbuf[:], KRc_g.rearrange("(j r) c -> r j c", j=4), [t_gg["KRc"][0]], krbuf.toks)
        nxt = attn_loads(0, "C")
        for h_ in range(H):
            K, V, Q = nxt
            Z = load_sz(20 + h_)
            if h_ + 1 < H:
                nxt = attn_loads(h_ + 1, "C")
            n = len(STEPS)
            st = {}
            abufs = a_b + [sp_b[0]]
            for t in range(n + 2):
                if t < n:
                    c, G, j, first, last = STEPS[t]
                    c0, N, qs, masked = geom(c, G)
                    sb_ = rot("Sb", 2)
                    mm(ps[sb_][:, 0:N], K[:, j, G * 128:(G + 1) * 128], Q[:, 0, qs], True, False, K.toks + Q.toks, [t_ps[sb_]], sgc=True)
                    mm(ps[sb_][:, 0:N], krbuf[:, j, G * 128:(G + 1) * 128], Q[0:64, 1, qs], False, not masked, krbuf.toks + Q.toks, [t_ps[sb_]], sgc=True)
                    if masked:
                        mm(ps[sb_][:, 0:128], ident_bf[:], maskC[:, j, :], False, True, ident_bf.toks + maskC.toks, [t_ps[sb_]], sgc=True)
                    A_ = abufs[rot("a_c", 3)]
                    act(A_[:, 0:N], ps[sb_][:, 0:N], AF.Exp, [t_ps[sb_]], A_.toks, scale=SCALE_C)
                    st[t] = A_
                if t >= 2:
                    c, G, j, first, last = STEPS[t - 2]
                    c0, N, qs, masked = geom(c, G)
                    A_ = st[t - 2]
                    mm(ps[5 + c][:, c0:512], V[:, j * 8 + G, :], A_[:, 0:N], first, last, V.toks + A_.toks, [t_ps[5 + c]], sgc=True)
                    mm(ps[2 + c][:, c0:512], ones_bf[:], A_[:, 0:N], first, last, ones_bf.toks + A_.toks, [t_ps[2 + c]], sgc=True)
            Y = ysb[rot("ysb", 2)]
            for c in range(2):
                cs = slice(c * 512, (c + 1) * 512)
                T1 = tm_f[rot("tm_f", 2)]
                recip(T1[:], ps[2 + c][:], [t_ps[2 + c]], T1.toks)
                tt("dve", Y[:, cs], ps[5 + c][:], T1[:], ALU.mult, [t_ps[5 + c]] + T1.toks, Y.toks, partial=(c > 0))
            gate_out(l, 20 + h_, [Y[:, 0:512], Y[:, 512:1024]], [Y.toks, Y.toks], h_ == 0, Z)
        mixer_rstd(2, 1536.0)

    def out_proj(l, xsrc, xsrc_toks, xdst, xdst_toks):
        wv_all = w_out[l].rearrange("(kc p) n -> p kc n", p=128)
        groups = [(0, 8), (8, 20), (20, 32)]
        xv = xsrc.rearrange("(i p) n -> p i n", p=128)
        for cb in range(D // WB):
            wv, wt = wload(wv_all[:, :, cb * WB:(cb + 1) * WB])
            XA = xres[rot("xres", 2)]
            dma("sp", XA[:], xv[:, :, cb * WB:(cb + 1) * WB], list(xsrc_toks) if xsrc_toks else [], XA.toks)
            for i in range(NT):
                O = onew[rot("onew", 2)]
                prev = XA[:, i, :]
                prev_toks = XA.toks
                for m, (k0, k1) in enumerate(groups):
                    b = rot("opb", 6)
                    for kc in range(k0, k1):
                        mm(ps[b][:, 0:WB], hT[:, kc, i * 128:(i + 1) * 128], wv[:, kc, :], kc == k0, kc == k1 - 1, hT.toks + wt.toks, [t_ps[b]])
                    stt(O[:], ps[b][:, 0:WB], rmc[m][:, i:i + 1], prev, ALU.mult, ALU.add,
                        [t_ps[b]] + rmc[m].toks + prev_toks, O.toks)
                    prev = O[:]
                    prev_toks = O.toks
                dma("sp", xdst[i * 128:(i + 1) * 128, cb * WB:(cb + 1) * WB], O[:], O.toks, [xdst_toks[i]], partial=True)

    def final_norm(xsrc, xsrc_toks):
        dma("sp", gfin_b[:], gfin_in[0].partition_broadcast(128), [], gfin_b.toks)
        for i in range(NT):
            X = xt[i % 2]
            dma("sp", X[:], xsrc[i * 128:(i + 1) * 128, :], [xsrc_toks[i]], X.toks)
            xt_t_toks[0] = X.toks
            rmsnorm_stats(X[:], i, D)
            stt(X[:], X[:], rstc[:, i:i + 1], gfin_b[:], ALU.mult, ALU.mult, X.toks + rstc.toks + gfin_b.toks, X.toks)
            dma("sp", y_out[i * 128:(i + 1) * 128, :], X[:], X.toks, [t_y], partial=True)

    dumps = []

    def dump_dram(name, ap, toks):
        o_ = nc.dram_tensor("dbg_" + name, list(ap.shape), ap.dtype, kind="ExternalOutput").ap()
        tk = Tok("dbg_" + name)
        dma("sp", o_, ap, toks, [tk])
        dumps.append(tk)

    def dump_sb(name, tile, ap=None):
        ap = tile[:] if ap is None else ap
        dump_dram(name, ap, tile.toks)

    def body():
        for l in range(DEPTH):
            xsrc = x_in if l == 0 else xs[l - 1]
            xsrc_toks = None if l == 0 else t_xs[l - 1]
            if stop == 0:
                dump_sb("cos2", cos2); dump_sb("sins", sins)
                return
            phase_A1(l, xsrc, xsrc_toks)
            if stop == 1:
                dump_sb("hT", hT)
                return
            phase_A2_full(l)
            if stop == 2 and nblk is not None:
                if nblk >= 3:
                    dump_sb("cqT", cqT)
                if nblk >= 4:
                    dump_sb("rq_b", rq_b)
                if nblk >= 5:
                    dump_sb("ckvT", ckvT)
                if nblk >= 6:
                    dump_sb("rkv_b", rkv_b); dump_sb("rkvc", rkvc)
                    dump_dram("KRc_i", KRc_i, t_gi["KRc"])
                if nblk >= 7:
                    dump_dram("QTcn", QTcn, t_QTcn); dump_dram("QTcr", QTcr, t_QTcr)
                    dump_dram("KTc_i0", KTc_i[0], [t_gi["KTc"][0]]); dump_dram("Vc_i0", Vc_i[0], [t_gi["Vc"][0]])
                return
            if stop == 2:
                if dbg:
                    for nm, ap, tk in (("QTb", QTb, t_QTb), ("NQTb", NQTb, t_QTb), ("QTcn", QTcn, t_QTcn), ("QTcr", QTcr, t_QTcr),
                                       ("uT", uT, t_uT), ("vn", vn, [t_vn]), ("sz", sz, t_sz),
                                       ("KTb_g0", KTb_g[0], [t_gg["KTb"][0]]), ("Vb_g1", Vb_g[1], [t_gg["Vb"][1]]), ("KTc_g2", KTc_g[2], [t_gg["KTc"][2]]),
                                       ("KRc_g", KRc_g, t_gg["KRc"]), ("Vc_g0", Vc_g[0], [t_gg["Vc"][0]])):
                        dump_dram(nm, ap, list(tk))
                return
            mixer_A(l)
            if stop == 3:
                dump_sb("ygT", hT); dump_sb("rmcA", rmc[0])
                return
            mixer_B(l)
            if stop == 4:
                dump_sb("ygT", hT); dump_sb("rmcA", rmc[0]); dump_sb("rmcB", rmc[1])
                return
            mixer_C(l)
            if stop == 5:
                dump_sb("ygT", hT); dump_sb("rmcA", rmc[0]); dump_sb("rmcB", rmc[1]); dump_sb("rmcC", rmc[2])
                return
            out_proj(l, xsrc, xsrc_toks, xs[l], t_xs[l])
            if stop == 6:
                dump_dram("xs0", xs[0], t_xs[0])
                return
        final_norm(xs[DEPTH - 1], t_xs[DEPTH - 1])

    body()
    if stop is not None:
        X = xt[0]
        dma("sp", X[:], x_in[0:128, :], [], X.toks)
        dma("sp", y_out[0:128, :], X[:], X.toks, [t_y])
    P.op("sp", None, reads=[t_y] + dumps)

    es = contextlib.ExitStack()
    P.emit(nc, es)
    es.close()
    return nc, P


_CACHE = {}


def _host_inputs(x, positions, g_pre, w_in, a_g_v, a_w_s, a_b_s, c_g_q, c_g_kv, c_w_uq, c_w_ukv, g_out, w_out, g_final):
    f32 = np.float32
    x = np.asarray(x, f32); positions = np.asarray(positions, np.int32)
    w_in = np.asarray(w_in, f32)
    w_in_p = np.ascontiguousarray(w_in[:, :, PERM])
    uq = np.asarray(c_w_uq, f32).reshape(DEPTH, 768, H, 192)
    w_uqn = np.ascontiguousarray(uq[..., :128].reshape(DEPTH, 768, 1536))
    w_uqr = np.ascontiguousarray(uq[..., 128:].reshape(DEPTH, 768, 768))
    w_uqs = np.ascontiguousarray(np.concatenate([uq[..., 160:], uq[..., 128:160]], axis=-1).reshape(DEPTH, 768, 768))
    ukv = np.asarray(c_w_ukv, f32).reshape(DEPTH, 512, H, 256)
    w_ukk = np.ascontiguousarray(ukv[..., :128].reshape(DEPTH, 512, 1536))
    w_ukv = np.ascontiguousarray(ukv[..., 128:].reshape(DEPTH, 512, 1536))

    def cols(v, n):
        return np.ascontiguousarray(np.asarray(v, f32).reshape(DEPTH, n, 128).transpose(0, 2, 1))

    shared = {
        "invf": (1.0 / (np.float32(10000.0) ** (np.arange(0, 64, 2, dtype=f32) / np.float32(64)))).astype(f32)[np.r_[0:32, 0:32]].reshape(64, 1),
        "tinc": _bf(np.tril(np.ones((128, 128), f32))),
        "ident": np.eye(128, dtype=f32),
        "triu": np.triu(np.ones((128, 128), f32)),
        "gpre": cols(g_pre, 32), "gout": cols(g_out, 32), "gq": cols(c_g_q, 6), "gkv": cols(c_g_kv, 4),
        "agv": np.asarray(a_g_v, f32).reshape(DEPTH, 1, 1024),
        "abs": np.asarray(a_b_s, f32).reshape(DEPTH, 1, 1024),
        "wsT": np.ascontiguousarray(np.asarray(a_w_s, f32).transpose(0, 3, 1, 2)),
        "gfin": np.asarray(g_final, f32).reshape(1, D),
        "w_in": w_in_p, "w_uqn": w_uqn, "w_uqr": w_uqr, "w_uqs": w_uqs, "w_ukk": w_ukk, "w_ukv": w_ukv,
        "w_out": np.asarray(w_out, f32),
    }
    in_maps = []
    kk = np.arange(128)[:, None]
    tq = np.arange(128)[None, :]
    for c in range(NCORE):
        b, r = divmod(c, 4)
        xc = np.ascontiguousarray(x[b].reshape(NT, 4, 128, D)[:, r].reshape(TL, D))
        pc = np.ascontiguousarray(positions[b].reshape(NT, 4, 128)[:, r].reshape(1, TL))
        mB = np.zeros((128, 4, 128), f32); mC = np.zeros((128, 4, 128), f32)
        for j in range(4):
            if j < r:
                mB[:, j, :] = 1; mC[:, j, :] = 1
            elif j == r:
                mB[:, j, :] = (kk < tq); mC[:, j, :] = (kk <= tq)
        NEG = np.float32(-30000.0)
        m = dict(shared)
        m.update({"x": xc, "pos": pc, "maskB": _bf((1 - mB) * NEG), "maskC": _bf((1 - mC) * NEG), "maskP": _bf((1 - mB) * -NEG)})
        in_maps.append(m)
    return in_maps


def kernel(x, positions, g_pre, w_in, a_g_v, a_w_s, a_b_s, c_g_q, c_g_kv, c_w_uq, c_w_ukv, g_out, w_out, g_final):
    if "nc" not in _CACHE:
        _CACHE["nc"] = build()[0]
    nc = _CACHE["nc"]
    in_maps = _host_inputs(x, positions, g_pre, w_in, a_g_v, a_w_s, a_b_s, c_g_q, c_g_kv, c_w_uq, c_w_ukv, g_out, w_out, g_final)
    res = run_bass_kernel_spmd(nc, in_maps, core_ids=list(range(NCORE)))
    out = np.empty((2, S, D), np.float32)
    for c in range(NCORE):
        b, r = divmod(c, 4)
        out[b].reshape(NT, 4, 128, D)[:, r] = np.asarray(res.results[c]["y"], np.float32).reshape(NT, 128, D)
    return out
```

```python
import contextlib
import os
import numpy as np
import ml_dtypes
import concourse.bass as bass
import concourse.mybir as mybir
from concourse.bass_utils import run_bass_kernel_spmd

F32 = mybir.dt.float32
BF16 = mybir.dt.bfloat16
I32 = mybir.dt.int32
AF = mybir.ActivationFunctionType
ALU = mybir.AluOpType

NCORE = 8
S = 4096
D = 4096
DEPTH = 2
TL = 1024
NT = 8
H = 12
EPS = 1e-6
SCALE_B = 128 ** -0.5
SCALE_C = 192 ** -0.5
WB = 256


class Tok:
    __slots__ = ("name", "writers", "readers", "gdeps", "sem", "cnt", "excl")

    def __init__(self, name, excl=False):
        self.name = name
        self.excl = excl
        self.writers = []
        self.readers = []
        self.gdeps = []
        self.sem = None
        self.cnt = 0


class Ins:
    __slots__ = ("eng", "fn", "deps", "dma", "sig", "sem", "val", "tok", "idx", "cc")

    def __init__(self, eng, fn, dma, cc):
        self.eng = eng
        self.fn = fn
        self.deps = []
        self.dma = dma
        self.cc = cc
        self.sig = False
        self.sem = None
        self.val = 0
        self.tok = None
        self.idx = 0


ENGS = ("pe", "act", "dve", "pool", "sp")


class Prog:
    def __init__(self):
        self.streams = {e: [] for e in ENGS}
        self.n = 0

    def op(self, eng, fn, reads=(), writes=(), partial=False, dma=False, cc=False):
        ins = Ins(eng, fn, dma, cc)
        ins.idx = self.n
        self.n += 1
        deps = []
        for t in reads:
            deps.extend(t.writers)
            if t.excl:
                deps.extend(r for r in t.readers if r.eng != eng)
        for t in writes:
            if partial and not t.readers and t.writers:
                deps.extend(t.gdeps)
                t.writers.append(ins)
            else:
                g = t.readers + t.writers
                deps.extend(g)
                t.gdeps = g
                t.writers = [ins]
                t.readers = []
        for t in reads:
            t.readers.append(ins)
        if dma or cc:
            ins.tok = writes[0] if writes else reads[0]
        seen = set()
        best = {}
        for d in deps:
            if d is ins or id(d) in seen:
                continue
            seen.add(id(d))
            if d.dma or d.cc:
                ins.deps.append(d)
                d.sig = True
                continue
            if d.eng == eng and eng == "pe":
                continue
            o = best.get(d.eng)
            if o is None or o.idx < d.idx:
                best[d.eng] = d
        for d in best.values():
            ins.deps.append(d)
            d.sig = True
        self.streams[eng].append(ins)
        return ins

    def emit(self, nc, es):
        eng_sem = {e: es.enter_context(nc.semaphore(f"s_{e}")) for e in ENGS}
        allins = sorted((i for e in ENGS for i in self.streams[e]), key=lambda i: i.idx)
        cnt = {e: 0 for e in ENGS}
        RING = {"sp": 44, "pool": 24, "act": 4, "dve": 2, "pe": 2}
        rings = {}
        dcount = {e: 0 for e in ENGS}
        nsem = 5
        for ins in allins:
            if ins.dma:
                e = ins.eng
                if e not in rings:
                    rings[e] = [es.enter_context(nc.semaphore(f"r_{e}{i}")) for i in range(RING[e])]
                    nsem += RING[e]
                k = dcount[e]
                dcount[e] += 1
                n = RING[e]
                ins.sem = rings[e][k % n]
                ins.val = 16 * (k // n + 1)
                ins.sig = True
                continue
            if not ins.sig:
                continue
            if ins.cc:
                t = ins.tok
                if t.sem is None:
                    t.sem = es.enter_context(nc.semaphore(f"c_{t.name}"))
                    nsem += 1
                t.cnt += 1
                ins.sem = t.sem
                ins.val = t.cnt
            else:
                cnt[ins.eng] += 1
                ins.sem = eng_sem[ins.eng]
                ins.val = cnt[ins.eng]
        self.nsem = nsem
        self.cnt = cnt
        self.dcount = dcount
        block = es.enter_context(nc.Block())

        def run(e, h):
            waited = {}
            for ins in self.streams[e]:
                need = {}
                for d in ins.deps:
                    k = id(d.sem)
                    if k not in need or need[k][1] < d.val:
                        need[k] = (d.sem, d.val)
                for k, (s, v) in need.items():
                    if waited.get(k, 0) >= v:
                        continue
                    h.wait_ge(s, v)
                    waited[k] = v
                if ins.fn is None:
                    continue
                if ins.dma and ins.val > 16 and waited.get(id(ins.sem), 0) < ins.val - 16:
                    h.wait_ge(ins.sem, ins.val - 16)
                    waited[id(ins.sem)] = ins.val - 16
                r = ins.fn(h)
                if ins.sig:
                    r.then_inc(ins.sem, 16 if ins.dma else 1)
            if e in rings:
                n = len(rings[e])
                for i, sm in enumerate(rings[e]):
                    uses = (dcount[e] - i + n - 1) // n if dcount[e] > i else 0
                    if uses > 0 and waited.get(id(sm), 0) < 16 * uses:
                        h.wait_ge(sm, 16 * uses)

        @block.tensor
        def _(h):
            run("pe", h)

        @block.scalar
        def _(h):
            run("act", h)

        @block.vector
        def _(h):
            run("dve", h)

        @block.gpsimd
        def _(h):
            run("pool", h)

        @block.sync
        def _(h):
            run("sp", h)


O_U, O_V, O_ZA, O_QB, O_KB, O_VB, O_ZB, O_CQ, O_CKV, O_KR, O_ZC = (
    0, 1024, 2048, 3072, 4608, 6144, 7680, 9216, 9984, 10496, 10560)


def _inproj_plan():
    perm = []
    blocks = []

    def add(kind, start, n):
        i = 0
        while n > 0:
            w = min(WB, n)
            blocks.append((kind, len(perm), w, i))
            perm.extend(range(start, start + w))
            start += w
            n -= w
            i += 1

    add("cq", O_CQ, 768)
    add("ckv", O_CKV, 512)
    blocks.append(("kr", len(perm), 128, 0))
    perm.extend(range(O_KR, O_KR + 64))
    perm.extend(range(O_KR + 32, O_KR + 64))
    perm.extend(range(O_KR, O_KR + 32))
    add("kb", O_KB, 1536)
    add("vb", O_VB, 1536)
    add("qb", O_QB, 1536)
    add("va", O_V, 1024)
    add("ua", O_U, 1024)
    add("za", O_ZA, 1024)
    add("zb", O_ZB, 1536)
    add("zc", O_ZC, 1536)
    return np.array(perm, dtype=np.int64), blocks


PERM, BLOCKS = _inproj_plan()
DINP = len(PERM)


def _bf(a):
    return np.asarray(a, dtype=np.float32).astype(ml_dtypes.bfloat16)


def build(stop=None, dbg=False, nblk=None, fakecc=False):
    nc = bass.Bass("TRN2", target_bir_lowering=False)
    P = Prog()
    KD = set(os.environ.get("KDBG", "").split(","))

    def din(name, shape, dt):
        return nc.dram_tensor(name, list(shape), dt, kind="ExternalInput").ap()

    def dscr(name, shape, dt):
        return nc.dram_tensor(name, list(shape), dt).ap()

    x_in = din("x", [TL, D], F32)
    pos_in = din("pos", [1, TL], I32)
    invf_in = din("invf", [64, 1], F32)
    maskB_in = din("maskB", [128, 4, 128], BF16)
    maskC_in = din("maskC", [128, 4, 128], BF16)
    maskP_in = din("maskP", [128, 4, 128], BF16)
    tinc_in = din("tinc", [128, 128], BF16)
    ident_in = din("ident", [128, 128], F32)
    triu_in = din("triu", [128, 128], F32)
    gpre_in = din("gpre", [DEPTH, 128, 32], F32)
    gout_in = din("gout", [DEPTH, 128, 32], F32)
    gq_in = din("gq", [DEPTH, 128, 6], F32)
    gkv_in = din("gkv", [DEPTH, 128, 4], F32)
    agv_in = din("agv", [DEPTH, 1, 1024], F32)
    abs_in = din("abs", [DEPTH, 1, 1024], F32)
    wsT_in = din("wsT", [DEPTH, 128, 8, 128], F32)
    gfin_in = din("gfin", [1, D], F32)
    WD = DEPTH if stop is None else 1
    NBLK = len(BLOCKS) if nblk is None else nblk
    WCOLS = BLOCKS[NBLK - 1][1] + BLOCKS[NBLK - 1][2]
    w_in = din("w_in", [WD, D, WCOLS] if (stop is None or stop >= 2) else [1, 128, 128], F32)
    w_uqn = din("w_uqn", [WD, 768, 1536], F32)
    w_uqr = din("w_uqr", [WD, 768, 768], F32)
    w_uqs = din("w_uqs", [WD, 768, 768], F32)
    w_ukk = din("w_ukk", [WD, 512, 1536], F32)
    w_ukv = din("w_ukv", [WD, 512, 1536], F32)
    w_out = din("w_out", [WD, D, D] if (stop is None or stop >= 6) else [1, 128, 128], F32)
    y_out = nc.dram_tensor("y", [TL, D], F32, kind="ExternalOutput").ap()

    xs = [dscr(f"xs{l}", [TL, D], F32) for l in range(DEPTH)]
    QTb = dscr("QTb", [1536, TL], BF16)
    NQTb = dscr("NQTb", [1536, TL], BF16)
    QTcn = dscr("QTcn", [1536, TL], BF16)
    QTcr = dscr("QTcr", [768, TL], BF16)
    uT = dscr("uT", [1024, TL], F32)
    vn = dscr("vn", [TL, 1024], BF16)
    sz = dscr("sz", [4096, TL], F32)
    KTb_i = [dscr(f"KTb_i{k}", [512, TL], BF16) for k in range(3)]
    Vb_i = [dscr(f"Vb_i{k}", [TL, 512], BF16) for k in range(3)]
    KTc_i = [dscr(f"KTc_i{k}", [512, TL], BF16) for k in range(3)]
    KRc_i = dscr("KRc_i", [64, TL], BF16)
    Vc_i = [dscr(f"Vc_i{k}", [TL, 512], BF16) for k in range(3)]
    KTb_g = [dscr(f"KTb_g{k}", [4 * 512, TL], BF16) for k in range(3)]
    Vb_g = [dscr(f"Vb_g{k}", [4 * TL, 512], BF16) for k in range(3)]
    KTc_g = [dscr(f"KTc_g{k}", [4 * 512, TL], BF16) for k in range(3)]
    KRc_g = dscr("KRc_g", [4 * 64, TL], BF16)
    Vc_g = [dscr(f"Vc_g{k}", [4 * TL, 512], BF16) for k in range(3)]

    t_xs = [[Tok(f"xs{l}_{i}") for i in range(NT)] for l in range(DEPTH)]
    t_QTb = [Tok(f"QTb{h}") for h in range(H)]
    t_QTcn = [Tok(f"QTcn{h}") for h in range(H)]
    t_QTcr = [Tok(f"QTcr{h}") for h in range(H)]
    t_uT = [Tok(f"uT{g}") for g in range(8)]
    t_vn = Tok("vn")
    t_sz = [Tok(f"sz{c}") for c in range(32)]
    t_gi = {k: [Tok(f"{k}_i{c}") for c in range(3)] for k in ("KTb", "Vb", "KTc", "Vc")}
    t_gg = {k: [Tok(f"{k}_g{c}") for c in range(3)] for k in ("KTb", "Vb", "KTc", "Vc")}
    t_gi["KRc"] = [Tok("KRc_i")]
    t_gg["KRc"] = [Tok("KRc_g")]
    t_y = Tok("y")

    BASE = 16512
    TOP = 229344
    SLOT = 1024
    nslots = (TOP - BASE + SLOT - 1) // SLOT
    arena = [Tok(f"sb{i}") for i in range(nslots)]

    class Tile:
        def __init__(self, name, shape, dt, off, toks=None):
            nbytes = int(np.prod(shape[1:])) * (4 if dt in (F32, I32) else 2)
            assert off % 32 == 0 and BASE <= off and off + nbytes <= TOP, (name, off, nbytes)
            self.t = nc.alloc_sbuf_tensor_at(name, list(shape), dt, offset=off)
            if toks is None:
                a = (off - BASE) // SLOT
                b = (off + nbytes - 1 - BASE) // SLOT
                toks = arena[a:b + 1]
            self.toks = list(toks)
            self.end = off + nbytes

        def __getitem__(self, k):
            return self.t[k]

    o = BASE
    hT = Tile("hT", [128, 32, TL], BF16, o, toks=[Tok("hT")]); o = hT.end
    wsl = []
    for i in range(2):
        wsl.append(Tile(f"wsl{i}", [128, 8192], BF16, o, toks=[Tok(f"wsl{i}")])); o = wsl[-1].end
    R3 = o

    ctop = TOP
    def ctile(name, shape, dt):
        nonlocal ctop
        nbytes = int(np.prod(shape[1:])) * (4 if dt in (F32, I32) else 2)
        nbytes = (nbytes + 31) // 32 * 32
        ctop -= nbytes
        return Tile(name, shape, dt, ctop, toks=[Tok(name)])

    ones_bf = ctile("ones_bf", [128, 128], BF16)
    ones_f = ctile("ones_f", [128, 128], F32)
    ident = ctile("ident", [128, 128], F32)
    tinc = ctile("tinc", [128, 128], BF16)
    e0 = ctile("e0", [128, 1], F32)
    maskB = ctile("maskB", [128, 4, 128], BF16)
    maskC = ctile("maskC", [128, 4, 128], BF16)
    maskP = ctile("maskP", [128, 4, 128], BF16)
    ident_bf = ctile("ident_bf", [128, 128], BF16)
    triu = ctile("triu", [128, 128], F32)
    gpre = ctile("gpre", [128, DEPTH, 32], F32)
    gout = ctile("gout", [128, DEPTH, 32], F32)
    gq = ctile("gq", [128, DEPTH, 6], F32)
    gkv = ctile("gkv", [128, DEPTH, 4], F32)
    invf = ctile("invf", [64, 1], F32)
    ssqc = ctile("ssqc", [128, 16], F32)
    rstc = ctile("rstc", [128, 16], F32)
    rkvc = ctile("rkvc", [128, 8], F32)
    rmc = [ctile(f"rmc{m}", [128, 8], F32) for m in range(3)]
    cos2 = ctile("cos2", [64, TL], F32)
    sins = ctile("sins", [64, TL], F32)
    R3END = ctop // 32 * 32

    def r3(name, shape, dt, off):
        t = Tile(name, shape, dt, R3 + off)
        assert t.end <= R3END, (name, t.end, R3END)
        return t

    xt = [r3(f"xt{i}", [128, D], F32, i * 16384) for i in range(2)]
    cqT = r3("cqT", [128, 6, TL], BF16, 32768)
    ckvT = r3("ckvT", [128, 4, TL], BF16, 45056)
    rq_b = r3("rq_b", [128, TL], F32, 53248)
    rkv_b = r3("rkv_b", [128, TL], F32, 57344)
    CR = r3("CR", [64, TL], F32, 61440)
    SR = r3("SR", [64, TL], F32, 65536)
    stf = [r3(f"stf{i}", [128, 512], F32, 69632 + 2048 * i) for i in range(3)]
    stb = [r3(f"stb{i}", [128, 512], BF16, 75776 + 1024 * i) for i in range(4)]
    tmf = [r3(f"tmf{i}", [128, 512], F32, 79872 + 2048 * i) for i in range(3)]
    agv_b = r3("agv_b", [128, 1024], F32, 86016)
    junk = r3("junk", [128, D], BF16, 90112)
    sc = [r3(f"sc{i}", [64, TL], F32, 16384 + 4096 * i) for i in range(4)]
    kbuf = [r3(f"kbuf{i}", [128, 4, TL], BF16, 0 + 8192 * i) for i in range(2)]
    vbuf = [r3(f"vbuf{i}", [128, 32, 128], BF16, 16384 + 8192 * i) for i in range(2)]
    qbuf = [r3(f"qbuf{i}", [128, 2, TL], BF16, 32768 + 4096 * i) for i in range(2)]
    krbuf = r3("krbuf", [64, 4, TL], BF16, 40960)
    e_f = [r3(f"e_f{i}", [128, 512], F32, 49152 + 2048 * i) for i in range(2)]
    sp_b = [r3(f"sp_b{i}", [128, 512], BF16, 53248 + 1024 * i) for i in range(2)]
    a_b = [r3(f"a_b{i}", [128, 512], BF16, 55296 + 1024 * i) for i in range(2)]
    tm_f = [r3(f"tm_f{i}", [128, 512], F32, 61440 + 2048 * i) for i in range(2)]
    tm3 = tm_f + [r3("tm_f2", [128, 512], F32, 57344)]
    carry = [r3(f"carry{i}", [128, 512], F32, 65536 + 2048 * i) for i in range(2)]
    sqacc = r3("sqacc", [128, TL], F32, 69632)
    szt = [r3(f"szt{i}", [128, TL], F32, 73728 + 4096 * i) for i in range(2)]
    ysb = [r3(f"ysb{i}", [128, TL], F32, 81920 + 4096 * i) for i in range(2)]
    sqt = [r3(f"sqt{i}", [128, 512], F32, 90112 + 2048 * i) for i in range(2)]
    vnt = r3("vnt", [128, NT, 1024], BF16, 0)
    wsT = r3("wsT", [128, 8, 128], BF16, 16384)
    wsTf = r3("wsTf", [128, 8, 128], F32, 20480)
    bsb = r3("bsb", [128, 8, 128], F32, 24576)
    xres = [r3(f"xres{i}", [128, NT, WB], F32, 0 + 8192 * i) for i in range(2)]
    onew = [r3(f"onew{i}", [128, WB], F32, 16384 + 1024 * i) for i in range(2)]
    rm_b = r3("rm_b", [128, TL], F32, 94208)
    gfin_b = r3("gfin_b", [128, D], F32, 32768)

    ps = [nc.alloc_psum_tensor(f"ps{b}", [128, 512], F32) for b in range(8)]
    t_ps = [Tok(f"ps{b}", excl=True) for b in range(8)]

    rr = {}

    def rot(key, n):
        v = rr.get(key, 0)
        rr[key] = v + 1
        return v % n

    def dma(eng, out, in_, reads, writes, partial=False):
        P.op(eng, lambda h: h.dma_start(out=out, in_=in_), reads=reads, writes=writes, partial=partial, dma=True)

    def mm(out, lhsT, rhs, start, stop, reads, writes, sgc=False):
        P.op("pe", lambda h: h.matmul(out, lhsT, rhs, start=start, stop=stop, skip_group_check=sgc), reads=reads, writes=writes, partial=True)

    def act(out, in_, func, reads, writes, scale=None, bias=None, accum_out=None, partial=False):
        kw = {}
        if scale is not None:
            kw["scale"] = scale
        if bias is not None:
            kw["bias"] = bias
        if accum_out is not None:
            kw["accum_out"] = accum_out
        P.op("act", lambda h: h.activation(out=out, in_=in_, func=func, **kw), reads=reads, writes=writes, partial=partial)

    def tt(eng, out, in0, in1, op, reads, writes, partial=False):
        P.op(eng, lambda h: h.tensor_tensor(out=out, in0=in0, in1=in1, op=op), reads=reads, writes=writes, partial=partial)

    def ts(eng, out, in0, s1, op0, reads, writes, s2=None, op1=None, partial=False):
        if op1 is None:
            P.op(eng, lambda h: h.tensor_scalar(out=out, in0=in0, scalar1=s1, scalar2=None, op0=op0), reads=reads, writes=writes, partial=partial)
        else:
            P.op(eng, lambda h: h.tensor_scalar(out=out, in0=in0, scalar1=s1, scalar2=s2, op0=op0, op1=op1), reads=reads, writes=writes, partial=partial)

    def stt(out, in0, scalar, in1, op0, op1, reads, writes, partial=False):
        P.op("dve", lambda h: h.scalar_tensor_tensor(out=out, in0=in0, scalar=scalar, in1=in1, op0=op0, op1=op1),
             reads=reads, writes=writes, partial=partial)

    def recip(out, in_, reads, writes):
        P.op("dve", lambda h: h.reciprocal(out=out, in_=in_), reads=reads, writes=writes)

    def memset(eng, ap, val, writes):
        P.op(eng, lambda h: h.memset(ap, val), writes=writes)

    def copy(eng, out, in_, reads, writes, partial=False):
        P.op(eng, lambda h: h.tensor_copy(out=out, in_=in_), reads=reads, writes=writes, partial=partial)

    def wload(src3):
        kc, n = src3.shape[1], src3.shape[2]
        assert kc * n <= 8192
        s = rot("wsl", 2)
        t = wsl[s]
        view = t.t[:, 0:kc * n].rearrange("p (k n) -> p k n", n=n)
        hk = max(1, kc // 2)
        first = True
        for k0 in range(0, kc, hk):
            k1 = min(kc, k0 + hk)
            dma("pool", view[:, k0:k1, :], src3[:, k0:k1, :], reads=[], writes=t.toks, partial=not first)
            first = False
        return view, t

    dma("sp", ident[:], ident_in, [], ident.toks)
    dma("sp", tinc[:], tinc_in, [], tinc.toks)
    dma("sp", maskB[:], maskB_in, [], maskB.toks)
    dma("sp", maskC[:], maskC_in, [], maskC.toks)
    dma("sp", maskP[:], maskP_in, [], maskP.toks)
    copy("dve", ident_bf[:], ident[:], ident.toks, ident_bf.toks)
    dma("sp", triu[:], triu_in, [], triu.toks)
    dma("sp", invf[:], invf_in, [], invf.toks)
    for l in range(DEPTH):
        dma("sp", gpre[:, l, :], gpre_in[l], [], gpre.toks, partial=(l > 0))
        dma("sp", gout[:, l, :], gout_in[l], [], gout.toks, partial=(l > 0))
        dma("sp", gq[:, l, :], gq_in[l], [], gq.toks, partial=(l > 0))
        dma("sp", gkv[:, l, :], gkv_in[l], [], gkv.toks, partial=(l > 0))
    memset("dve", ones_bf[:], 1.0, ones_bf.toks)
    memset("dve", ones_f[:], 1.0, ones_f.toks)
    memset("dve", e0[:], 0.0, e0.toks)
    P.op("dve", lambda h: h.memset(e0[0:1, :], 1.0), writes=e0.toks)

    posi_t = Tile("posi", [64, TL], I32, R3 + 0)
    dma("sp", posi_t[:], pos_in[0].partition_broadcast(64), [], posi_t.toks)
    ang, kf, rr_, mm_ = sc
    copy("dve", ang[:], posi_t[:], posi_t.toks, ang.toks)
    ts("dve", ang[:], ang[:], invf[:, 0:1], ALU.mult, ang.toks + invf.toks, ang.toks)
    ts("dve", kf[:], ang[:], float(1.0 / (2 * np.pi)), ALU.mult, ang.toks, kf.toks)
    ki = Tile("ki", [64, TL], I32, R3 + 4096)
    copy("dve", ki[:], kf[:], kf.toks, ki.toks)
    copy("dve", kf[:], ki[:], ki.toks, kf.toks)
    C1 = 6.28125
    C2 = float(np.float32(2 * np.pi - C1))
    C3 = float(2 * np.pi - C1 - C2)
    stt(rr_[:], kf[:], -C1, ang[:], ALU.mult, ALU.add, kf.toks + ang.toks, rr_.toks)
    stt(rr_[:], kf[:], -C2, rr_[:], ALU.mult, ALU.add, kf.toks + rr_.toks, rr_.toks)
    stt(rr_[:], kf[:], -C3, rr_[:], ALU.mult, ALU.add, kf.toks + rr_.toks, rr_.toks)
    PI = float(np.pi)

    def wrap(t):
        ts("dve", mm_[:], t[:], PI, ALU.is_gt, t.toks, mm_.toks, s2=-2 * PI, op1=ALU.mult)
        tt("dve", t[:], t[:], mm_[:], ALU.add, t.toks + mm_.toks, t.toks)
        ts("dve", mm_[:], t[:], -PI, ALU.is_lt, t.toks, mm_.toks, s2=2 * PI, op1=ALU.mult)
        tt("dve", t[:], t[:], mm_[:], ALU.add, t.toks + mm_.toks, t.toks)

    wrap(rr_)
    act(sins[:], rr_[:], AF.Sin, rr_.toks, sins.toks)
    ts("dve", rr_[:], rr_[:], PI / 2, ALU.add, rr_.toks, rr_.toks)
    wrap(rr_)
    act(cos2[:], rr_[:], AF.Sin, rr_.toks, cos2.toks)
    ts("dve", sins[0:32, :], sins[0:32, :], -1.0, ALU.mult, sins.toks, sins.toks)

    def rmsnorm_stats(xt_t, col, width, eng_sq="act"):
        act(junk[:, 0:width], xt_t, AF.Square, xt_t_toks[0], junk.toks + ssqc.toks, accum_out=ssqc[:, col:col + 1])
        ts("dve", rstc[:, col:col + 1], ssqc[:, col:col + 1], 1.0 / width, ALU.mult, ssqc.toks, rstc.toks, s2=EPS, op1=ALU.add)
        act(rstc[:, col:col + 1], rstc[:, col:col + 1], AF.Sqrt, rstc.toks, rstc.toks)
        recip(rstc[:, col:col + 1], rstc[:, col:col + 1], rstc.toks, rstc.toks)

    xt_t_toks = [None]

    def phase_A1(l, xsrc, xsrc_toks):
        for i in range(NT):
            s = i % 2
            X = xt[s]
            dma("sp", X[:], xsrc[i * 128:(i + 1) * 128, :], [xsrc_toks[i]] if xsrc_toks else [], X.toks)
            xt_t_toks[0] = X.toks
            rmsnorm_stats(X[:], i, D)
            ts("dve", X[:], X[:], rstc[:, i:i + 1], ALU.mult, X.toks + rstc.toks, X.toks)
            for j in range(8):
                b = 2 + (j % 2)
                for q in range(4):
                    kc = 4 * j + q
                    P.op("pe", lambda h, b=b, q=q, kc=kc, X=X: h.transpose(out=ps[b][:, q * 128:(q + 1) * 128],
                                                                     in_=X[:, kc * 128:(kc + 1) * 128], identity=ident[:]),
                         reads=X.toks + ident.toks, writes=[t_ps[b]], partial=True)
                for q in range(4):
                    kc = 4 * j + q
                    src = ps[b][:, q * 128:(q + 1) * 128]
                    dst = hT[:, kc, i * 128:(i + 1) * 128]
                    ts("dve", dst, src, gpre[:, l, kc:kc + 1], ALU.mult, [t_ps[b]] + gpre.toks, hT.toks, partial=True)

    def acc_bank():
        return rot("acc", 4)

    def st_f():
        return stf[rot("stf", 3)]

    def st_b():
        return stb[rot("stb", 4)]

    def tm():
        return tmf[rot("tmf", 3)]

    def bcast_rstd(sqb0, width, dst):
        for th in range(2):
            cs = slice(th * 512, (th + 1) * 512)
            t1 = tm()
            act(t1[:], ps[sqb0 + th][:], AF.Sqrt, [t_ps[sqb0 + th]], t1.toks, scale=1.0 / width, bias=EPS)
            recip(dst[:, cs], t1[:], t1.toks, dst.toks)

    def tokmajor_cols(src_b, dstc):
        for i in range(NT):
            mm(ps[7][:, i:i + 1], src_b[:, i * 128:(i + 1) * 128], e0[:], True, True, src_b.toks + e0.toks, [t_ps[7]])
        copy("dve", dstc[:, 0:NT], ps[7][:, 0:NT], [t_ps[7]], dstc.toks)

    def phase_A2_full(l):
        wv_all = w_in[l].rearrange("(kc p) n -> p kc n", p=128)
        dma("sp", agv_b[:], agv_in[l, 0].partition_broadcast(128), [], agv_b.toks)
        for (kind, c0, ncols, bi) in BLOCKS[:NBLK]:
            if kind == "ckv" and bi == 0:
                bcast_rstd(4, 768.0, rq_b)
                tt("dve", CR[:], rq_b[0:64, :], cos2[:], ALU.mult, rq_b.toks + cos2.toks, CR.toks)
                tt("dve", SR[:], rq_b[0:64, :], sins[:], ALU.mult, rq_b.toks + sins.toks, SR.toks)
            if kind == "kr":
                bcast_rstd(4, 512.0, rkv_b)
                tokmajor_cols(rkv_b, rkvc)
            if kind == "kb" and bi == 0:
                mla_upproj(l)
            _blk(l, wv_all, kind, c0, ncols, bi)
            if kind == "vb" and bi == 5:
                gathers()

    def _blk(l, wv_all, kind, c0, ncols, bi):
        wv, wt = wload(wv_all[:, :, c0:c0 + ncols])
        if kind in ("vb", "va"):
            for i in range(NT):
                b = acc_bank()
                for kc in range(32):
                    mm(ps[b][:, 0:ncols], hT[:, kc, i * 128:(i + 1) * 128], wv[:, kc, :], kc == 0, kc == 31,
                       hT.toks + wt.toks, [t_ps[b]])
                if kind == "vb":
                    sb = st_b()
                    if i % 2 == 0:
                        act(sb[:, 0:ncols], ps[b][:, 0:ncols], AF.Copy, [t_ps[b]], sb.toks)
                    else:
                        copy("dve", sb[:, 0:ncols], ps[b][:, 0:ncols], [t_ps[b]], sb.toks)
                    vc0 = bi * WB
                    dma("sp", Vb_i[vc0 // 512][i * 128:(i + 1) * 128, vc0 % 512:vc0 % 512 + ncols], sb[:, 0:ncols], sb.toks, [t_gi["Vb"][vc0 // 512]], partial=True)
                else:
                    gt = tm()
                    act(gt[:, 0:ncols], ps[b][:, 0:ncols], AF.Gelu_apprx_tanh, [t_ps[b]], gt.toks)
                    sb = st_b()
                    for g2 in range(ncols // 128):
                        col = 8 + g2
                        sl = slice(g2 * 128, (g2 + 1) * 128)
                        act(junk[:, sl], gt[:, sl], AF.Square, gt.toks, junk.toks + ssqc.toks, accum_out=ssqc[:, col:col + 1])
                        ts("dve", rstc[:, col:col + 1], ssqc[:, col:col + 1], 1.0 / 128, ALU.mult, ssqc.toks, rstc.toks, s2=EPS, op1=ALU.add)
                        act(rstc[:, col:col + 1], rstc[:, col:col + 1], AF.Sqrt, rstc.toks, rstc.toks)
                        recip(rstc[:, col:col + 1], rstc[:, col:col + 1], rstc.toks, rstc.toks)
                        gcol = bi * WB + g2 * 128
                        stt(sb[:, sl], gt[:, sl], rstc[:, col:col + 1], agv_b[:, gcol:gcol + 128], ALU.mult, ALU.mult,
                            gt.toks + rstc.toks + agv_b.toks, sb.toks, partial=(g2 > 0))
                    dma("sp", vn[i * 128:(i + 1) * 128, bi * WB:bi * WB + ncols], sb[:, 0:ncols], sb.toks, [t_vn], partial=True)
            return
        if kind == "kr":
            for th in range(2):
                cs = slice(th * 512, (th + 1) * 512)
                bA = acc_bank()
                for kc in range(32):
                    mm(ps[bA][0:64, :], wv[:, kc, 0:64], hT[:, kc, cs], kc == 0, kc == 31, hT.toks + wt.toks, [t_ps[bA]])
                bB = acc_bank()
                for kc in range(32):
                    mm(ps[bB][0:64, :], wv[:, kc, 64:128], hT[:, kc, cs], kc == 0, kc == 31, hT.toks + wt.toks, [t_ps[bB]])
                t1 = tm(); t2 = tm(); sb = st_b()
                tt("dve", t1[0:64, :], ps[bA][0:64, :], cos2[:, cs], ALU.mult, [t_ps[bA]] + cos2.toks, t1.toks)
                tt("dve", t2[0:64, :], ps[bB][0:64, :], sins[:, cs], ALU.mult, [t_ps[bB]] + sins.toks, t2.toks)
                tt("dve", sb[0:64, :], t1[0:64, :], t2[0:64, :], ALU.add, t1.toks + t2.toks, sb.toks)
                dma("sp", KRc_i[:, cs], sb[0:64, :], sb.toks, [t_gi["KRc"][0]], partial=True)
            return
        for m in range(ncols // 128):
            fcol = bi * WB + m * 128
            fch = fcol // 128
            for th in range(2):
                cs = slice(th * 512, (th + 1) * 512)
                b = acc_bank()
                for kc in range(32):
                    mm(ps[b][:], wv[:, kc, m * 128:(m + 1) * 128], hT[:, kc, cs], kc == 0, kc == 31, hT.toks + wt.toks, [t_ps[b]])
                src = ps[b][:]
                if kind in ("cq", "ckv"):
                    dstT = cqT if kind == "cq" else ckvT
                    gg = gq if kind == "cq" else gkv
                    sqb = 4 + th
                    nch = 6 if kind == "cq" else 4
                    sq = tm()
                    if "nossq" not in KD:
                        act(sq[:], src, AF.Square, [t_ps[b]], sq.toks)
                        if "nomm" not in KD:
                            mm(ps[sqb][:], ones_f[:], sq[:], fch == 0, fch == nch - 1, ones_f.toks + sq.toks, [t_ps[sqb]])
                    if "nocqT" not in KD:
                        ts("dve", dstT[:, fch, cs], src, gg[:, l, fch:fch + 1], ALU.mult, [t_ps[b]] + gg.toks, dstT.toks, partial=True)
                    else:
                        copy("dve", dstT[:, fch, cs], src, [t_ps[b]], dstT.toks, partial=True)
                elif kind == "kb":
                    sb = st_b()
                    if th == 0:
                        act(sb[:], src, AF.Copy, [t_ps[b]], sb.toks)
                    else:
                        copy("dve", sb[:], src, [t_ps[b]], sb.toks)
                    dma("sp", KTb_i[fch // 4][(fch % 4) * 128:(fch % 4 + 1) * 128, cs], sb[:], sb.toks, [t_gi["KTb"][fch // 4]], partial=True)
                elif kind == "qb":
                    sb = st_b(); sb2 = st_b()
                    act(sb[:], src, AF.Copy, [t_ps[b]], sb.toks)
                    act(sb2[:], src, AF.Copy, [t_ps[b]], sb2.toks, scale=-SCALE_B)
                    dma("sp", QTb[fcol:fcol + 128, cs], sb[:], sb.toks, [t_QTb[fch]], partial=True)
                    dma("sp", NQTb[fcol:fcol + 128, cs], sb2[:], sb2.toks, [t_QTb[fch]], partial=True)
                elif kind == "ua":
                    sf = st_f()
                    act(sf[:], src, AF.Gelu_apprx_tanh, [t_ps[b]], sf.toks)
                    dma("sp", uT[fcol:fcol + 128, cs], sf[:], sf.toks, [t_uT[fch]], partial=True)
                else:
                    base = {"za": 0, "zb": 1024, "zc": 2560}[kind]
                    sf = st_f()
                    act(sf[:], src, AF.Silu, [t_ps[b]], sf.toks)
                    r0 = base + fcol
                    dma("sp", sz[r0:r0 + 128, cs], sf[:], sf.toks, [t_sz[r0 // 128]], partial=True)

    def mla_upproj(l):
        for blk in range(2):
            wv, wt = wload(w_uqn[l].rearrange("(kc p) n -> p kc n", p=128)[:, :, blk * 768:(blk + 1) * 768])
            for hh in range(6):
                h_ = blk * 6 + hh
                for th in range(2):
                    cs = slice(th * 512, (th + 1) * 512)
                    b = acc_bank()
                    for kc in range(6):
                        mm(ps[b][:], wv[:, kc, hh * 128:(hh + 1) * 128], cqT[:, kc, cs], kc == 0, kc == 5, cqT.toks + wt.toks, [t_ps[b]])
                    sb = st_b()
                    tt("dve", sb[:], ps[b][:], rq_b[:, cs], ALU.mult, [t_ps[b]] + rq_b.toks, sb.toks)
                    dma("sp", QTcn[h_ * 128:(h_ + 1) * 128, cs], sb[:], sb.toks, [t_QTcn[h_]], partial=True)
        wr, wrt = wload(w_uqr[l].rearrange("(kc p) n -> p kc n", p=128))
        wsw, wswt = wload(w_uqs[l].rearrange("(kc p) n -> p kc n", p=128))
        for h_ in range(H):
            for th in range(2):
                cs = slice(th * 512, (th + 1) * 512)
                bA = acc_bank()
                for kc in range(6):
                    mm(ps[bA][0:64, :], wr[:, kc, h_ * 64:(h_ + 1) * 64], cqT[:, kc, cs], kc == 0, kc == 5, cqT.toks + wrt.toks, [t_ps[bA]])
                bB = acc_bank()
                for kc in range(6):
                    mm(ps[bB][0:64, :], wsw[:, kc, h_ * 64:(h_ + 1) * 64], cqT[:, kc, cs], kc == 0, kc == 5, cqT.toks + wswt.toks, [t_ps[bB]])
                t1 = tm(); t2 = tm(); sb = st_b()
                tt("dve", t1[0:64, :], ps[bA][0:64, :], CR[:, cs], ALU.mult, [t_ps[bA]] + CR.toks, t1.toks)
                tt("dve", t2[0:64, :], ps[bB][0:64, :], SR[:, cs], ALU.mult, [t_ps[bB]] + SR.toks, t2.toks)
                tt("dve", sb[0:64, :], t1[0:64, :], t2[0:64, :], ALU.add, t1.toks + t2.toks, sb.toks)
                dma("sp", QTcr[h_ * 64:(h_ + 1) * 64, cs], sb[0:64, :], sb.toks, [t_QTcr[h_]], partial=True)
        for blk in range(2):
            wv, wt = wload(w_ukk[l].rearrange("(kc p) n -> p kc n", p=128)[:, :, blk * 768:(blk + 1) * 768])
            for hh in range(6):
                h_ = blk * 6 + hh
                for th in range(2):
                    cs = slice(th * 512, (th + 1) * 512)
                    b = acc_bank()
                    for kc in range(4):
                        mm(ps[b][:], wv[:, kc, hh * 128:(hh + 1) * 128], ckvT[:, kc, cs], kc == 0, kc == 3, ckvT.toks + wt.toks, [t_ps[b]])
                    sb = st_b()
                    tt("dve", sb[:], ps[b][:], rkv_b[:, cs], ALU.mult, [t_ps[b]] + rkv_b.toks, sb.toks)
                    dma("sp", KTc_i[h_ // 4][(h_ % 4) * 128:(h_ % 4 + 1) * 128, cs], sb[:], sb.toks, [t_gi["KTc"][h_ // 4]], partial=True)
        for blk in range(2):
            wv, wt = wload(w_ukv[l].rearrange("(kc p) n -> p kc n", p=128)[:, :, blk * 768:(blk + 1) * 768])
            for i in range(NT):
                for n3 in range(3):
                    b = acc_bank()
                    for kc in range(4):
                        mm(ps[b][:, 0:256], ckvT[:, kc, i * 128:(i + 1) * 128], wv[:, kc, n3 * 256:(n3 + 1) * 256], kc == 0, kc == 3,
                           ckvT.toks + wt.toks, [t_ps[b]])
                    sb = st_b()
                    ts("dve", sb[:, 0:256], ps[b][:, 0:256], rkvc[:, i:i + 1], ALU.mult, [t_ps[b]] + rkvc.toks, sb.toks)
                    c0 = blk * 768 + n3 * 256
                    dma("sp", Vc_i[c0 // 512][i * 128:(i + 1) * 128, c0 % 512:c0 % 512 + 256], sb[:, 0:256], sb.toks, [t_gi["Vc"][c0 // 512]], partial=True)

    RG = [[0, 1, 2, 3], [4, 5, 6, 7]]

    def gathers():
        lst = []
        for k, src, dst in (("KTb", KTb_i, KTb_g), ("Vb", Vb_i, Vb_g), ("KTc", KTc_i, KTc_g), ("Vc", Vc_i, Vc_g)):
            for c in range(3):
                lst.append((t_gi[k][c], t_gg[k][c], src[c], dst[c]))
        lst.insert(6, (t_gi["KRc"][0], t_gg["KRc"][0], KRc_i, KRc_g))
        for ti, tg, src, dst in lst:
            if fakecc:
                rows = src.shape[0]
                for j in range(4):
                    dma("sp", dst[j * rows:(j + 1) * rows, :], src, [ti], [tg], partial=(j > 0))
                continue
            P.op("pool", lambda h, src=src, dst=dst: h.collective_compute("AllGather", ALU.bypass, replica_groups=RG, ins=[src], outs=[dst], dma_qos="P3"),
                 reads=[ti], writes=[tg], cc=True)

    def load_sz(fc):
        Z = szt[rot("szt", 2)]
        dma("sp", Z[:], sz[fc * 128:(fc + 1) * 128, :], [t_sz[fc]], Z.toks)
        return Z

    def gate_out(l, fc, srcs, src_toks, first_of_mixer, Z):
        for th in range(2):
            cs = slice(th * 512, (th + 1) * 512)
            stt(hT[:, fc, cs], srcs[th], gout[:, l, fc:fc + 1], Z[:, cs], ALU.mult, ALU.mult,
                src_toks[th] + gout.toks + Z.toks, hT.toks, partial=True)
            if first_of_mixer:
                act(sqacc[:, cs], srcs[th], AF.Square, src_toks[th], sqacc.toks, partial=(th > 0))
            else:
                q = sqt[rot("sqt", 2)]
                act(q[:], srcs[th], AF.Square, src_toks[th], q.toks)
                tt("pool", sqacc[:, cs], sqacc[:, cs], q[:], ALU.add, sqacc.toks + q.toks, sqacc.toks)

    def mixer_rstd(m, width):
        for th in range(2):
            cs = slice(th * 512, (th + 1) * 512)
            mm(ps[7][:], ones_f[:], sqacc[:, cs], True, True, ones_f.toks + sqacc.toks, [t_ps[7]])
            t1 = tm_f[rot("tm_f", 2)]
            act(t1[:], ps[7][:], AF.Sqrt, [t_ps[7]], t1.toks, scale=1.0 / width, bias=EPS)
            recip(rm_b[:, cs], t1[:], t1.toks, rm_b.toks)
        tokmajor_cols(rm_b, rmc[m])

    def mixer_A(l):
        dma("sp", vnt[:], vn.rearrange("(i p) f -> p i f", p=128), [t_vn], vnt.toks)
        dma("sp", wsTf[:], wsT_in[l], [], wsTf.toks)
        dma("sp", bsb[:], abs_in[l, 0].partition_broadcast(128).rearrange("p (g t) -> p g t", g=8), [], bsb.toks)
        for g in range(8):
            tt("dve", wsT[:, g, :], wsTf[:, g, :], triu[:], ALU.mult, wsTf.toks + triu.toks, wsT.toks, partial=(g > 0))
        for g in range(8):
            U = ysb[rot("ysb", 2)]
            dma("sp", U[:], uT[g * 128:(g + 1) * 128, :], [t_uT[g]], U.toks)
            Y = ysb[rot("ysb", 2)]
            for th in range(2):
                b = acc_bank()
                for q in range(4):
                    i = th * 4 + q
                    mm(ps[b][:, q * 128:(q + 1) * 128], vnt[:, i, g * 128:(g + 1) * 128], wsT[:, g, :], True, True,
                       vnt.toks + wsT.toks, [t_ps[b]])
                cs = slice(th * 512, (th + 1) * 512)
                t1 = tm_f[rot("tm_f", 2)]
                tt("dve", t1[:].rearrange("p (q t) -> p q t", q=4), ps[b][:].rearrange("p (q t) -> p q t", q=4),
                   bsb[:, g, :].unsqueeze(1).broadcast_to([128, 4, 128]), ALU.add, [t_ps[b]] + bsb.toks, t1.toks)
                tt("pool", Y[:, cs], t1[:], U[:, cs], ALU.mult, t1.toks + U.toks, Y.toks, partial=(th > 0))
            gate_out(l, g, [Y[:, 0:512], Y[:, 512:1024]], [Y.toks, Y.toks], g == 0, load_sz(g))
        mixer_rstd(0, 1024.0)

    def head_steps():
        st = []
        for c in range(2):
            ko = [(G, j) for G in range(4 * c + 3, -1, -1) for j in range(3, -1, -1)]
            for n_, (G, j) in enumerate(ko):
                st.append((c, G, j, n_ == 0, n_ == len(ko) - 1))
        return st

    STEPS = head_steps()

    def geom(c, G):
        ilo = max(G, 4 * c)
        c0 = (ilo - 4 * c) * 128
        return c0, 512 - c0, slice(c * 512 + c0, (c + 1) * 512), G >= 4 * c

    def attn_loads(h_, mixer):
        s = h_ % 2
        K = kbuf[s]; V = vbuf[s]; Q = qbuf[s]
        KT_g, V_g, kk, vk = (KTb_g, Vb_g, "KTb", "Vb") if mixer == "B" else (KTc_g, Vc_g, "KTc", "Vc")
        ch, hl = h_ // 4, h_ % 4
        dma("sp", K[:], KT_g[ch].rearrange("(j r) c -> r j c", j=4)[hl * 128:(hl + 1) * 128], [t_gg[kk][ch]], K.toks)
        Vv = V_g[ch].rearrange("(j g p) c -> p j g c", j=4, g=8)
        for j in range(4):
            dma("sp", V[:, j * 8:(j + 1) * 8, :], Vv[:, j, :, hl * 128:(hl + 1) * 128], [t_gg[vk][ch]], V.toks, partial=(j > 0))
        if mixer == "B":
            dma("sp", Q[:, 0, :], QTb[h_ * 128:(h_ + 1) * 128, :], [t_QTb[h_]], Q.toks)
            dma("sp", Q[:, 1, :], NQTb[h_ * 128:(h_ + 1) * 128, :], [t_QTb[h_]], Q.toks, partial=True)
        else:
            dma("sp", Q[:, 0, :], QTcn[h_ * 128:(h_ + 1) * 128, :], [t_QTcn[h_]], Q.toks)
            dma("sp", Q[0:64, 1, :], QTcr[h_ * 64:(h_ + 1) * 64, :], [t_QTcr[h_]], Q.toks, partial=True)
        return K, V, Q

    def mixer_B(l):
        nxt = attn_loads(0, "B")
        for h_ in range(H):
            K, V, Q = nxt
            Z = load_sz(8 + h_)
            if h_ + 1 < H:
                nxt = attn_loads(h_ + 1, "B")
            for c in range(2):
                memset("pool", carry[c][:], 0.0, carry[c].toks)
            n = len(STEPS)
            st = {}
            for t in range(n + 3):
                if t < n:
                    c, G, j, first, last = STEPS[t]
                    c0, N, qs, masked = geom(c, G)
                    sb_ = rot("Sb", 2)
                    mm(ps[sb_][:, 0:N], K[:, j, G * 128:(G + 1) * 128], Q[:, 0, qs], True, not masked, K.toks + Q.toks, [t_ps[sb_]], sgc=True)
                    if masked:
                        mm(ps[sb_][:, 0:128], ident_bf[:], maskB[:, j, :], False, True, ident_bf.toks + maskB.toks, [t_ps[sb_]], sgc=True)
                    E = e_f[rot("e_f", 2)]
                    act(E[:, 0:N], ps[sb_][:, 0:N], AF.Exp, [t_ps[sb_]], E.toks, scale=SCALE_B)
                    SP = sp_b[rot("sp_b", 2)]
                    act(SP[:, 0:N], E[:, 0:N], AF.Ln, E.toks, SP.toks, bias=1.0)
                    st[t] = [SP, None]
                if 1 <= t <= n:
                    c, G, j, first, last = STEPS[t - 1]
                    c0, N, qs, masked = geom(c, G)
                    SP = st[t - 1][0]
                    cb = 2 + rot("Cb", 2)
                    kt_ap = K[:, j, G * 128:(G + 1) * 128]
                    mm(ps[cb][:, 0:N], tinc[:], SP[:, 0:N], True, False, tinc.toks + SP.toks, [t_ps[cb]], sgc=True)
                    mm(ps[cb][:, 0:N], kt_ap, Q[:, 1, qs], False, not masked, K.toks + Q.toks, [t_ps[cb]], sgc=True)
                    if masked:
                        mm(ps[cb][:, 0:128], ident_bf[:], maskP[:, j, :], False, True, ident_bf.toks + maskP.toks, [t_ps[cb]], sgc=True)
                    mm(ps[4][:, 0:N], ones_bf[:], SP[:, 0:N], True, True, ones_bf.toks + SP.toks, [t_ps[4]])
                    T1 = tm3[rot("tm3", 3)]
                    CY = carry[c]
                    tt("dve", T1[:, 0:N], ps[cb][:, 0:N], CY[:, c0:512], ALU.add, [t_ps[cb]] + CY.toks, T1.toks)
                    tt("dve", CY[:, c0:512], ps[4][:, 0:N], CY[:, c0:512], ALU.add, [t_ps[4]] + CY.toks, CY.toks)
                    st[t - 1].append(T1)
                if 2 <= t <= n + 1:
                    c, G, j, first, last = STEPS[t - 2]
                    c0, N, qs, masked = geom(c, G)
                    T1 = st[t - 2][2]
                    A_ = a_b[rot("a_b", 2)]
                    act(A_[:, 0:N], T1[:, 0:N], AF.Exp, T1.toks, A_.toks, scale=-1.0)
                    st[t - 2][1] = A_
                if t >= 3:
                    c, G, j, first, last = STEPS[t - 3]
                    c0, N, qs, masked = geom(c, G)
                    A_ = st[t - 3][1]
                    mm(ps[5 + c][:, c0:512], V[:, j * 8 + G, :], A_[:, 0:N], first, last, V.toks + A_.toks, [t_ps[5 + c]], sgc=True)
            gate_out(l, 8 + h_, [ps[5][:], ps[6][:]], [[t_ps[5]], [t_ps[6]]], h_ == 0, Z)
        mixer_rstd(1, 1536.0)

    def mixer_C(l):
        dma("sp", krbuf[:], KRc_g.rearrange("(j r) c -> r j c", j=4), [t_gg["KRc"][0]], krbuf.toks)
        nxt = attn_loads(0, "C")
        for h_ in range(H):
            K, V, Q = nxt
            Z = load_sz(20 + h_)
            if h_ + 1 < H:
                nxt = attn_loads(h_ + 1, "C")
            n = len(STEPS)
            st = {}
            abufs = a_b + [sp_b[0]]
            for t in range(n + 2):
                if t < n:
                    c, G, j, first, last = STEPS[t]
                    c0, N, qs, masked = geom(c, G)
                    sb_ = rot("Sb", 2)
                    mm(ps[sb_][:, 0:N], K[:, j, G * 128:(G + 1) * 128], Q[:, 0, qs], True, False, K.toks + Q.toks, [t_ps[sb_]], sgc=True)
                    mm(ps[sb_][:, 0:N], krbuf[:, j, G * 128:(G + 1) * 128], Q[0:64, 1, qs], False, not masked, krbuf.toks + Q.toks, [t_ps[sb_]], sgc=True)
                    if masked:
                        mm(ps[sb_][:, 0:128], ident_bf[:], maskC[:, j, :], False, True, ident_bf.toks + maskC.toks, [t_ps[sb_]], sgc=True)
                    A_ = abufs[rot("a_c", 3)]
                    act(A_[:, 0:N], ps[sb_][:, 0:N], AF.Exp, [t_ps[sb_]], A_.toks, scale=SCALE_C)
                    st[t] = A_
                if t >= 2:
                    c, G, j, first, last = STEPS[t - 2]
                    c0, N, qs, masked = geom(c, G)
                    A_ = st[t - 2]
                    mm(ps[5 + c][:, c0:512], V[:, j * 8 + G, :], A_[:, 0:N], first, last, V.toks + A_.toks, [t_ps[5 + c]], sgc=True)
                    mm(ps[2 + c][:, c0:512], ones_bf[:], A_[:, 0:N], first, last, ones_bf.toks + A_.toks, [t_ps[2 + c]], sgc=True)
            Y = ysb[rot("ysb", 2)]
            for c in range(2):
                cs = slice(c * 512, (c + 1) * 512)
                T1 = tm_f[rot("tm_f", 2)]
                recip(T1[:], ps[2 + c][:], [t_ps[2 + c]], T1.toks)
                tt("dve", Y[:, cs], ps[5 + c][:], T1[:], ALU.mult, [t_ps[5 + c]] + T1.toks, Y.toks, partial=(c > 0))
            gate_out(l, 20 + h_, [Y[:, 0:512], Y[:, 512:1024]], [Y.toks, Y.toks], h_ == 0, Z)
        mixer_rstd(2, 1536.0)

    def out_proj(l, xsrc, xsrc_toks, xdst, xdst_toks):
        wv_all = w_out[l].rearrange("(kc p) n -> p kc n", p=128)
        groups = [(0, 8), (8, 20), (20, 32)]
        xv = xsrc.rearrange("(i p) n -> p i n", p=128)
        for cb in range(D // WB):
            wv, wt = wload(wv_all[:, :, cb * WB:(cb + 1) * WB])
            XA = xres[rot("xres", 2)]
            dma("sp", XA[:], xv[:, :, cb * WB:(cb + 1) * WB], list(xsrc_toks) if xsrc_toks else [], XA.toks)
            for i in range(NT):
                O = onew[rot("onew", 2)]
                prev = XA[:, i, :]
                prev_toks = XA.toks
                for m, (k0, k1) in enumerate(groups):
                    b = rot("opb", 6)
                    for kc in range(k0, k1):
                        mm(ps[b][:, 0:WB], hT[:, kc, i * 128:(i + 1) * 128], wv[:, kc, :], kc == k0, kc == k1 - 1, hT.toks + wt.toks, [t_ps[b]])
                    stt(O[:], ps[b][:, 0:WB], rmc[m][:, i:i + 1], prev, ALU.mult, ALU.add,
                        [t_ps[b]] + rmc[m].toks + prev_toks, O.toks)
                    prev = O[:]
                    prev_toks = O.toks
                dma("sp", xdst[i * 128:(i + 1) * 128, cb * WB:(cb + 1) * WB], O[:], O.toks, [xdst_toks[i]], partial=True)

    def final_norm(xsrc, xsrc_toks):
        dma("sp", gfin_b[:], gfin_in[0].partition_broadcast(128), [], gfin_b.toks)
        for i in range(NT):
            X = xt[i % 2]
            dma("sp", X[:], xsrc[i * 128:(i + 1) * 128, :], [xsrc_toks[i]], X.toks)
            xt_t_toks[0] = X.toks
            rmsnorm_stats(X[:], i, D)
            stt(X[:], X[:], rstc[:, i:i + 1], gfin_b[:], ALU.mult, ALU.mult, X.toks + rstc.toks + gfin_b.toks, X.toks)
            dma("sp", y_out[i * 128:(i + 1) * 128, :], X[:], X.toks, [t_y], partial=True)

    dumps = []

    def dump_dram(name, ap, toks):
        o_ = nc.dram_tensor("dbg_" + name, list(ap.shape), ap.dtype, kind="ExternalOutput").ap()
        tk = Tok("dbg_" + name)
        dma("sp", o_, ap, toks, [tk])
        dumps.append(tk)

    def dump_sb(name, tile, ap=None):
        ap = tile[:] if ap is None else ap
        dump_dram(name, ap, tile.toks)

    def body():
        for l in range(DEPTH):
            xsrc = x_in if l == 0 else xs[l - 1]
            xsrc_toks = None if l == 0 else t_xs[l - 1]
            if stop == 0:
                dump_sb("cos2", cos2); dump_sb("sins", sins)
                return
            phase_A1(l, xsrc, xsrc_toks)
            if stop == 1:
                dump_sb("hT", hT)
                return
            phase_A2_full(l)
            if stop == 2 and nblk is not None:
                if nblk >= 3:
                    dump_sb("cqT", cqT)
                if nblk >= 4:
                    dump_sb("rq_b", rq_b)
                if nblk >= 5:
                    dump_sb("ckvT", ckvT)
                if nblk >= 6:
                    dump_sb("rkv_b", rkv_b); dump_sb("rkvc", rkvc)
                    dump_dram("KRc_i", KRc_i, t_gi["KRc"])
                if nblk >= 7:
                    dump_dram("QTcn", QTcn, t_QTcn); dump_dram("QTcr", QTcr, t_QTcr)
                    dump_dram("KTc_i0", KTc_i[0], [t_gi["KTc"][0]]); dump_dram("Vc_i0", Vc_i[0], [t_gi["Vc"][0]])
                return
            if stop == 2:
                if dbg:
                    for nm, ap, tk in (("QTb", QTb, t_QTb), ("NQTb", NQTb, t_QTb), ("QTcn", QTcn, t_QTcn), ("QTcr", QTcr, t_QTcr),
                                       ("uT", uT, t_uT), ("vn", vn, [t_vn]), ("sz", sz, t_sz),
                                       ("KTb_g0", KTb_g[0], [t_gg["KTb"][0]]), ("Vb_g1", Vb_g[1], [t_gg["Vb"][1]]), ("KTc_g2", KTc_g[2], [t_gg["KTc"][2]]),
                                       ("KRc_g", KRc_g, t_gg["KRc"]), ("Vc_g0", Vc_g[0], [t_gg["Vc"][0]])):
                        dump_dram(nm, ap, list(tk))
                return
            mixer_A(l)
            if stop == 3:
                dump_sb("ygT", hT); dump_sb("rmcA", rmc[0])
                return
            mixer_B(l)
            if stop == 4:
                dump_sb("ygT", hT); dump_sb("rmcA", rmc[0]); dump_sb("rmcB", rmc[1])
                return
            mixer_C(l)
            if stop == 5:
                dump_sb("ygT", hT); dump_sb("rmcA", rmc[0]); dump_sb("rmcB", rmc[1]); dump_sb("rmcC", rmc[2])
                return
            out_proj(l, xsrc, xsrc_toks, xs[l], t_xs[l])
            if stop == 6:
                dump_dram("xs0", xs[0], t_xs[0])
                return
        final_norm(xs[DEPTH - 1], t_xs[DEPTH - 1])

    body()
    if stop is not None:
        X = xt[0]
        dma("sp", X[:], x_in[0:128, :], [], X.toks)
        dma("sp", y_out[0:128, :], X[:], X.toks, [t_y])
    P.op("sp", None, reads=[t_y] + dumps)

    es = contextlib.ExitStack()
    P.emit(nc, es)
    es.close()
    return nc, P


_CACHE = {}


def _host_inputs(x, positions, g_pre, w_in, a_g_v, a_w_s, a_b_s, c_g_q, c_g_kv, c_w_uq, c_w_ukv, g_out, w_out, g_final):
    f32 = np.float32
    x = np.asarray(x, f32); positions = np.asarray(positions, np.int32)
    w_in = np.asarray(w_in, f32)
    w_in_p = np.ascontiguousarray(w_in[:, :, PERM])
    uq = np.asarray(c_w_uq, f32).reshape(DEPTH, 768, H, 192)
    w_uqn = np.ascontiguousarray(uq[..., :128].reshape(DEPTH, 768, 1536))
    w_uqr = np.ascontiguousarray(uq[..., 128:].reshape(DEPTH, 768, 768))
    w_uqs = np.ascontiguousarray(np.concatenate([uq[..., 160:], uq[..., 128:160]], axis=-1).reshape(DEPTH, 768, 768))
    ukv = np.asarray(c_w_ukv, f32).reshape(DEPTH, 512, H, 256)
    w_ukk = np.ascontiguousarray(ukv[..., :128].reshape(DEPTH, 512, 1536))
    w_ukv = np.ascontiguousarray(ukv[..., 128:].reshape(DEPTH, 512, 1536))

    def cols(v, n):
        return np.ascontiguousarray(np.asarray(v, f32).reshape(DEPTH, n, 128).transpose(0, 2, 1))

    shared = {
        "invf": (1.0 / (np.float32(10000.0) ** (np.arange(0, 64, 2, dtype=f32) / np.float32(64)))).astype(f32)[np.r_[0:32, 0:32]].reshape(64, 1),
        "tinc": _bf(np.tril(np.ones((128, 128), f32))),
        "ident": np.eye(128, dtype=f32),
        "triu": np.triu(np.ones((128, 128), f32)),
        "gpre": cols(g_pre, 32), "gout": cols(g_out, 32), "gq": cols(c_g_q, 6), "gkv": cols(c_g_kv, 4),
        "agv": np.asarray(a_g_v, f32).reshape(DEPTH, 1, 1024),
        "abs": np.asarray(a_b_s, f32).reshape(DEPTH, 1, 1024),
        "wsT": np.ascontiguousarray(np.asarray(a_w_s, f32).transpose(0, 3, 1, 2)),
        "gfin": np.asarray(g_final, f32).reshape(1, D),
        "w_in": w_in_p, "w_uqn": w_uqn, "w_uqr": w_uqr, "w_uqs": w_uqs, "w_ukk": w_ukk, "w_ukv": w_ukv,
        "w_out": np.asarray(w_out, f32),
    }
    in_maps = []
    kk = np.arange(128)[:, None]
    tq = np.arange(128)[None, :]
    for c in range(NCORE):
        b, r = divmod(c, 4)
        xc = np.ascontiguousarray(x[b].reshape(NT, 4, 128, D)[:, r].reshape(TL, D))
        pc = np.ascontiguousarray(positions[b].reshape(NT, 4, 128)[:, r].reshape(1, TL))
        mB = np.zeros((128, 4, 128), f32); mC = np.zeros((128, 4, 128), f32)
        for j in range(4):
            if j < r:
                mB[:, j, :] = 1; mC[:, j, :] = 1
            elif j == r:
                mB[:, j, :] = (kk < tq); mC[:, j, :] = (kk <= tq)
        NEG = np.float32(-30000.0)
        m = dict(shared)
        m.update({"x": xc, "pos": pc, "maskB": _bf((1 - mB) * NEG), "maskC": _bf((1 - mC) * NEG), "maskP": _bf((1 - mB) * -NEG)})
        in_maps.append(m)
    return in_maps


def kernel(x, positions, g_pre, w_in, a_g_v, a_w_s, a_b_s, c_g_q, c_g_kv, c_w_uq, c_w_ukv, g_out, w_out, g_final):
    if "nc" not in _CACHE:
        _CACHE["nc"] = build()[0]
    nc = _CACHE["nc"]
    in_maps = _host_inputs(x, positions, g_pre, w_in, a_g_v, a_w_s, a_b_s, c_g_q, c_g_kv, c_w_uq, c_w_ukv, g_out, w_out, g_final)
    res = run_bass_kernel_spmd(nc, in_maps, core_ids=list(range(NCORE)))
    out = np.empty((2, S, D), np.float32)
    for c in range(NCORE):
        b, r = divmod(c, 4)
        out[b].reshape(NT, 4, 128, D)[:, r] = np.asarray(res.results[c]["y"], np.float32).reshape(NT, 128, D)
    return out
```
